# Optimizing a Trainium2 kernel written in Bass

```python
import jax, jax.numpy as jnp
from jax import lax
import numpy as np

D_MODEL = 1024
BATCH = 32
SEQ = 2048
DEPTH = 1
DEC_BATCH = 16
DEC_SEQ = 64
PAST_LEN = 1024

CHUNK = 64
N_META = 16
GLA_HEADS = 4
GLA_DK = D_MODEL // 2
GLA_DV = D_MODEL
GLA_DK_HEAD = GLA_DK // GLA_HEADS
GLA_DV_HEAD = GLA_DV // GLA_HEADS
GATE_RANK = 16
GATE_TAU = 16.0
POOL_WIDTH = D_MODEL // 2
POOL_WINDOWS = (2, 4, 8, 16)
POOL_GROUPS = len(POOL_WINDOWS)
POOL_GROUP = POOL_WIDTH // POOL_GROUPS
POOL_OUT_GROUP = D_MODEL // POOL_GROUPS
POOL_BUF = max(POOL_WINDOWS) - 1
D_FF = -(-8 * D_MODEL // (3 * 256)) * 256
IN_SPLITS = (GLA_DK, GLA_DK, GLA_DV, GLA_DV, GATE_RANK, POOL_WIDTH, D_MODEL, D_MODEL)
D_IN = sum(IN_SPLITS)
EPS = 1e-6

kernel_name = 'gla_pool_gated_hybrid_stream_step'


def _rmsnorm(x, g):
    xf = x.astype(jnp.float32)
    y = xf * lax.rsqrt(jnp.mean(xf * xf, axis=-1, keepdims=True) + EPS)
    return (y * g.astype(jnp.float32)).astype(x.dtype)


def _gla_scan(q, k, v, g, S0):
    C = q.shape[2]
    causal = jnp.tril(jnp.ones((C, C), dtype=bool))

    def step(S, inp):
        qc, kc, vc, gc = inp
        b = jnp.cumsum(gc, axis=1)
        o_inter = jnp.einsum('bthk,bhkv->bthv', qc * jnp.exp(b), S)
        diff = b[:, :, None] - b[:, None]
        decay = jnp.exp(jnp.where(causal[None, :, :, None, None], diff, -jnp.inf))
        attn = jnp.einsum('bthk,bshk,btshk->bhts', qc, kc, decay)
        o_intra = jnp.einsum('bhts,bshv->bthv', attn, vc)
        b_last = b[:, -1]
        k_dec = kc * jnp.exp(b_last[:, None] - b)
        S_new = jnp.exp(b_last)[..., None] * S + jnp.einsum('bshk,bshv->bhkv', k_dec, vc)
        return S_new, o_inter + o_intra

    S, o = lax.scan(step, S0, (q, k, v, g))
    return o, S


def _gla(q, k, v, log_a, S0, front_pad):
    B, L = q.shape[0], q.shape[1]
    total = front_pad + L
    n_chunks = -(-total // CHUNK)
    end_pad = n_chunks * CHUNK - total

    def blocks(t):
        t = jnp.pad(t.astype(jnp.float32), ((0, 0), (front_pad, end_pad), (0, 0), (0, 0)))
        t = t.reshape(B, n_chunks, CHUNK, t.shape[2], t.shape[3])
        return jnp.moveaxis(t, 1, 0)

    o, S = _gla_scan(blocks(q), blocks(k), blocks(v), blocks(log_a), S0.astype(jnp.float32))
    o = jnp.moveaxis(o, 0, 1).reshape(B, n_chunks * CHUNK, GLA_HEADS, GLA_DV_HEAD)
    return o[:, front_pad:front_pad + L], S


def _pool_mix(p, buf, pos0):
    B, L, W = p.shape
    z = jnp.concatenate([buf, p], axis=1)
    cs = jnp.concatenate([jnp.zeros((B, 1, W), jnp.float32), jnp.cumsum(z, axis=1)], axis=1)
    pos = pos0 + jnp.arange(L)
    outs = []
    for gi, w in enumerate(POOL_WINDOWS):
        lo, hi = gi * POOL_GROUP, (gi + 1) * POOL_GROUP
        s = cs[:, POOL_BUF + 1:POOL_BUF + 1 + L, lo:hi] - cs[:, POOL_BUF + 1 - w:POOL_BUF + 1 - w + L, lo:hi]
        cnt = jnp.minimum(w, pos + 1).astype(jnp.float32)[None, :, None]
        outs.append(s / cnt - p[..., lo:hi])
    mixed = jnp.stack(outs, axis=2)
    return mixed, z[:, -POOL_BUF:]


def _mixer(xn, S0, pool_buf, pool_pos0, front_pad, w_in, w_alpha2, b_alpha, gla_norm_g, w_pool, pool_scale, w_out):
    B, L, _ = xn.shape
    proj = xn @ w_in
    cuts = [int(c) for c in np.cumsum(IN_SPLITS)[:-1]]
    q, k, v, r, a_lr, p, g_a, g_b = jnp.split(proj, cuts, axis=-1)
    q = q.reshape(B, L, GLA_HEADS, GLA_DK_HEAD) * (GLA_DK_HEAD ** -0.5)
    k = k.reshape(B, L, GLA_HEADS, GLA_DK_HEAD)
    v = v.reshape(B, L, GLA_HEADS, GLA_DV_HEAD)
    log_a = jax.nn.log_sigmoid((a_lr @ w_alpha2 + b_alpha).astype(jnp.float32)) / GATE_TAU
    log_a = log_a.reshape(B, L, GLA_HEADS, GLA_DK_HEAD)
    o, S = _gla(q, k, v, log_a, S0, front_pad)
    o = o * lax.rsqrt(jnp.mean(o * o, axis=-1, keepdims=True) + EPS)
    o = o.reshape(B, L, GLA_DV) * gla_norm_g.astype(jnp.float32)
    y_a = jax.nn.silu(r.astype(jnp.float32)) * o
    mixed, new_buf = _pool_mix(p.astype(jnp.float32), pool_buf.astype(jnp.float32), pool_pos0)
    y_b = jnp.einsum('blgc,gcd->blgd', mixed, w_pool.astype(jnp.float32)).reshape(B, L, D_MODEL)
    y_b = y_b * pool_scale.astype(jnp.float32)
    merged = jax.nn.sigmoid(g_a.astype(jnp.float32)) * y_a + jax.nn.sigmoid(g_b.astype(jnp.float32)) * y_b
    out = (merged @ w_out.astype(jnp.float32)).astype(xn.dtype)
    return out, S, new_buf


def _layer(x, S0, pool_buf, pool_pos0, front_pad, n_drop, norm1_g, w_in, w_alpha2, b_alpha, gla_norm_g,
           w_pool, pool_scale, w_out, norm2_g, w_ffn_gate, w_ffn_up, w_ffn_down, norm_f_g):
    mix, S, buf = _mixer(_rmsnorm(x, norm1_g), S0, pool_buf, pool_pos0, front_pad,
                         w_in, w_alpha2, b_alpha, gla_norm_g, w_pool, pool_scale, w_out)
    h = (x + mix)[:, n_drop:]
    hn = _rmsnorm(h, norm2_g)
    h = h + (jax.nn.silu(hn @ w_ffn_gate) * (hn @ w_ffn_up)) @ w_ffn_down
    return _rmsnorm(h, norm_f_g), S, buf


def setup_inputs(seed: int = 0) -> dict:
    key = jax.random.key(seed)
    ks = jax.random.split(key, 18)

    def n(k, shape, scale):
        return jax.random.normal(k, shape, jnp.float32) * scale

    return {
        'x_prompt': n(ks[0], (BATCH, SEQ, D_MODEL), 1.0),
        'x_sample': n(ks[1], (DEC_BATCH, DEC_SEQ, D_MODEL), 1.0),
        'state_gla': n(ks[2], (DEC_BATCH, GLA_HEADS, GLA_DK_HEAD, GLA_DV_HEAD), 1.0),
        'state_pool': n(ks[3], (DEC_BATCH, POOL_BUF, POOL_WIDTH), 1.0),
        'meta_tokens': n(ks[4], (N_META, D_MODEL), 1.0),
        'norm1_g': 1.0 + n(ks[5], (D_MODEL,), 0.02),
        'w_in': n(ks[6], (D_MODEL, D_IN), D_MODEL ** -0.5),
        'w_alpha2': n(ks[7], (GATE_RANK, GLA_DK), GATE_RANK ** -0.5),
        'b_alpha': n(ks[8], (GLA_DK,), 0.1),
        'gla_norm_g': 1.0 + n(ks[9], (GLA_DV,), 0.02),
        'w_pool': n(ks[10], (POOL_GROUPS, POOL_GROUP, POOL_OUT_GROUP), POOL_GROUP ** -0.5),
        'pool_scale': 1.0 + n(ks[11], (D_MODEL,), 0.1),
        'w_out': n(ks[12], (D_MODEL, D_MODEL), D_MODEL ** -0.5),
        'norm2_g': 1.0 + n(ks[13], (D_MODEL,), 0.02),
        'w_ffn_gate': n(ks[14], (D_MODEL, D_FF), D_MODEL ** -0.5),
        'w_ffn_up': n(ks[15], (D_MODEL, D_FF), D_MODEL ** -0.5),
        'w_ffn_down': n(ks[16], (D_FF, D_MODEL), D_FF ** -0.5),
        'norm_f_g': 1.0 + n(ks[17], (D_MODEL,), 0.02),
    }


def reference(x_prompt, x_sample, state_gla, state_pool, meta_tokens, norm1_g, w_in, w_alpha2, b_alpha,
              gla_norm_g, w_pool, pool_scale, w_out, norm2_g, w_ffn_gate, w_ffn_up, w_ffn_down, norm_f_g):
    B = x_prompt.shape[0]
    meta = jnp.broadcast_to(meta_tokens[None].astype(x_prompt.dtype), (B, N_META, D_MODEL))
    xp = jnp.concatenate([meta, x_prompt], axis=1)
    S0 = jnp.zeros((B, GLA_HEADS, GLA_DK_HEAD, GLA_DV_HEAD), jnp.float32)
    buf0 = jnp.zeros((B, POOL_BUF, POOL_WIDTH), jnp.float32)
    y_prompt, S_p, buf_p = _layer(xp, S0, buf0, 0, (-N_META) % CHUNK, N_META, norm1_g, w_in, w_alpha2, b_alpha,
                                  gla_norm_g, w_pool, pool_scale, w_out, norm2_g, w_ffn_gate, w_ffn_up,
                                  w_ffn_down, norm_f_g)
    y_sample, S_s, buf_s = _layer(x_sample, state_gla, state_pool, PAST_LEN, 0, 0, norm1_g, w_in, w_alpha2, b_alpha,
                                  gla_norm_g, w_pool, pool_scale, w_out, norm2_g, w_ffn_gate, w_ffn_up,
                                  w_ffn_down, norm_f_g)
    return (y_prompt, y_sample,
            S_p.astype(state_gla.dtype), buf_p.astype(state_pool.dtype),
            S_s.astype(state_gla.dtype), buf_s.astype(state_pool.dtype))
```

```python
import os
import numpy as np
from contextlib import ExitStack
import concourse.bass as bass
import concourse.mybir as mybir
from concourse.bass_utils import run_bass_kernel_spmd

F32 = mybir.dt.float32
BF16 = mybir.dt.bfloat16
AF = mybir.ActivationFunctionType
ALU = mybir.AluOpType

ENGS = ("pe", "act", "dve", "pool", "sp")

D = 1024
NB_SEQ = 4
SEQ = 2048
DFF = 2816
NFF = 22
EPS = 1e-6
QSCALE = 128 ** -0.5


def I(name, *args, **kwargs):
    return (name, args, kwargs)


class Res:
    __slots__ = ("name", "last_w", "readers")

    def __init__(self, name):
        self.name = name
        self.last_w = None
        self.readers = []


class Op:
    __slots__ = ("eng", "fn", "deps", "needs_inc", "ms", "dma", "dsem", "dval")


class Prog:
    def __init__(self, nc, same_engine_sync=True):
        self.nc = nc
        self.ops = {e: [] for e in ENGS}
        self.same_engine_sync = same_engine_sync
        self.dma_sems = {}

    def add(self, eng, fn, reads=(), writes=(), dma_key=None):
        op = Op()
        op.eng = eng
        op.fn = fn
        op.needs_inc = False
        op.ms = None
        op.dma = dma_key is not None
        op.dsem = None
        op.dval = None
        deps = []
        seen = set()

        def _add(d):
            if d is None or id(d) in seen:
                return
            seen.add(id(d))
            deps.append(d)

        for r in reads:
            _add(r.last_w)
        for w in writes:
            _add(w.last_w)
            for rd in w.readers:
                _add(rd)
        for r in reads:
            r.readers.append(op)
        for w in writes:
            w.last_w = op
            w.readers = []
        fdeps = []
        for d in deps:
            if d is op:
                continue
            if not d.dma and d.eng == eng:
                if eng == "pe" or not self.same_engine_sync:
                    continue
            fdeps.append(d)
        op.deps = fdeps
        for d in fdeps:
            d.needs_inc = True
        if op.dma:
            ent = self.dma_sems.setdefault(dma_key, [None, 0])
            ent[1] += 16
            op.dsem = dma_key
            op.dval = ent[1]
        self.ops[eng].append(op)
        return op

    def emit(self, stack, final_wait_eng="sp"):
        nc = self.nc
        esem = {e: stack.enter_context(nc.semaphore("s_" + e)) for e in ENGS}
        for k, ent in self.dma_sems.items():
            ent[0] = stack.enter_context(nc.semaphore("d_%s" % (k,)))
        for e in ENGS:
            c = 0
            for op in self.ops[e]:
                if op.dma:
                    continue
                if op.needs_inc:
                    c += 1
                    op.ms = c
        block = stack.enter_context(nc.Block())

        def make(e):
            def body(eng):
                waited = {}
                for op in self.ops[e]:
                    need = {}
                    for d in op.deps:
                        if d.dma:
                            sem, val = self.dma_sems[d.dsem][0], d.dval
                            key = ("d", d.dsem)
                        else:
                            sem, val = esem[d.eng], d.ms
                            key = ("e", d.eng)
                        if key not in need or need[key][1] < val:
                            need[key] = (sem, val)
                    for key, (sem, val) in need.items():
                        if waited.get(key, 0) < val:
                            eng.wait_ge(sem, val)
                            waited[key] = val
                    ins = op.fn(eng) if callable(op.fn) else getattr(eng, op.fn[0])(*op.fn[1], **op.fn[2])
                    if op.dma:
                        ins.then_inc(self.dma_sems[op.dsem][0], 16)
                    elif op.needs_inc:
                        ins.then_inc(esem[e], 1)
                if e == final_wait_eng:
                    for k, ent in self.dma_sems.items():
                        if waited.get(("d", k), 0) < ent[1]:
                            eng.wait_ge(ent[0], ent[1])
                    for e2 in ENGS:
                        c = max([o.ms or 0 for o in self.ops[e2]] + [0])
                        if c and e2 != e and waited.get(("e", e2), 0) < c:
                            eng.wait_ge(esem[e2], c)
            return body

        block.tensor(make("pe"))
        block.scalar(make("act"))
        block.vector(make("dve"))
        block.gpsimd(make("pool"))
        block.sync(make("sp"))


def _consts():
    s = np.arange(128)[:, None]
    t = np.arange(128)[None, :]
    same = (s // 64) == (t // 64)
    maskN = (s <= t).astype(np.float32)
    maskS = ((s <= t) & same).astype(np.float32)
    triN = maskN * (-1.0 / 16.0)
    triS = maskS * (-1.0 / 16.0)
    cf = np.concatenate([maskN, maskS, triN, triS], axis=1).astype(np.float32)
    mats = [np.eye(128, dtype=np.float32)]
    for w in (2, 4, 8, 16):
        mats.append(((s <= t) & (s > t - w)).astype(np.float32) / w - np.eye(128, dtype=np.float32))
    for w in (2, 4, 8, 16):
        mats.append(((s - 128) > (t - w)).astype(np.float32) / w)
    for w in (2, 4, 8, 16):
        mats.append(((s <= t) & (s > t - w) & same).astype(np.float32) / w - np.eye(128, dtype=np.float32))
    for w in (2, 4, 8, 16):
        m = np.zeros((128, 128), np.float32)
        m += (((s - 128) > (t - w)) & (t < 64) & (s >= 113)).astype(np.float32) / w
        m += (((s - 64) > (t - 64 - w)) & (t >= 64) & (s >= 49) & (s < 64)).astype(np.float32) / w
        mats.append(m)
    cb = np.concatenate(mats, axis=1).astype(np.float32)
    return cf, cb


C_Q, C_K, C_V, C_R, C_A, C_P, C_GA, C_GB = 0, 512, 1024, 2048, 3072, 3088, 3600, 4624
IN_SLABS = [C_Q, C_K, C_V, C_V + 512, C_P, C_R, C_R + 512, C_GA, C_GA + 512, C_GB, C_GB + 512]
N_SLABS = 28


def build_program(n_tiles_limit=None):
    nc = bass.Bass("TRN2", target_bir_lowering=False)

    def din(name, shape, dt=F32):
        return nc.dram_tensor(name, list(shape), dt, kind="ExternalInput").ap()

    def dout(name, shape):
        return nc.dram_tensor(name, list(shape), F32, kind="ExternalOutput").ap()

    xp = din("xp", [NB_SEQ, SEQ, D])
    xs = din("xs", [128, D])
    sg_in = din("sg_in", [2, 4, 128, 256])
    sp_in = din("sp_in", [2, 15, 512])
    meta = din("meta", [16, D])
    g1c = din("g1c", [128, 8])
    g2c = din("g2c", [128, 8])
    w_in = din("w_in", [D, 5648])
    w_a2 = din("w_a2", [16, 512])
    b_al = din("b_al", [1, 512])
    gla_g = din("gla_g", [D])
    w_pool = din("w_pool", [4, 128, 256])
    pscale = din("pscale", [D])
    w_out = din("w_out", [D, D])
    w_fg = din("w_fg", [D, DFF])
    w_fu = din("w_fu", [D, DFF])
    w_fd = din("w_fd", [DFF, D])
    gf = din("gf", [D])
    cst_f = din("cst_f", [128, 512])
    cst_b = din("cst_b", [128, 17 * 128])

    yp = dout("yp", [NB_SEQ, SEQ, D])
    ys = dout("ys", [128, D])
    sgp = dout("sgp", [NB_SEQ, 4, 128, 256])
    spp = dout("spp", [NB_SEQ, 15, 512])
    sgs = dout("sgs", [2, 4, 128, 256])
    sps = dout("sps", [2, 15, 512])

    wsl = nc.dram_tensor("wsl", [N_SLABS, 128, 8, 512], BF16).ap()

    P = Prog(nc)
    with ExitStack() as st:
        def sb(name, shape, dt):
            return st.enter_context(nc.sbuf_tensor(name, list(shape), dt))

        def ps(name, shape, dt):
            return st.enter_context(nc.psum_tensor(name, list(shape), dt))

        slab = [sb("slab%d" % i, [128, 8, 512], BF16) for i in range(3)]
        wout_sb = sb("wout_sb", [128, 8, 1024], BF16)
        wpool_sb = sb("wpool_sb", [128, 4, 256], BF16)
        wa_sb = sb("wa_sb", [128, 8, 16], BF16)
        w2aug = sb("w2aug", [33, 512], BF16)
        xh2 = [sb("xhA", [128, 4, 1024], F32), sb("xhB", [128, 4, 1024], F32)]
        xb2 = [sb("xb0", [128, 1024], BF16), sb("xb1", [128, 1024], BF16)]
        ybuf_t = sb("ybuf", [128, 512], F32)
        junk = sb("junk", [128, 1024], BF16)
        ssA = sb("ssA", [128, 4], F32)
        rstdA = sb("rstdA", [128, 4], F32)
        xnT = sb("xnT", [128, 8, 512], BF16)
        qTt = sb("qTt", [128, 4, 512], BF16)
        kTt = sb("kTt", [128, 4, 512], BF16)
        regX = sb("regX", [128, 12288], BF16)
        regY = sb("regY", [128, 4096], F32)
        gt = sb("gt", [128, 512], F32)
        a_aug = sb("a_aug", [33, 512], BF16)
        p_sb = sb("p_sb", [128, 5, 512], BF16)
        p_meta = sb("p_meta", [128, 512], BF16)
        p_hist = sb("p_hist", [128, 512], BF16)
        S_f = [sb("S_f%d" % i, [128, 4, 256], F32) for i in range(3)]
        S_b = [sb("S_b%d" % i, [128, 4, 256], BF16) for i in range(3)]
        ident = sb("ident", [128, 128], BF16)
        attn4 = sb("attn4", [128, 4, 512], BF16)
        ktok4 = sb("ktok4", [128, 4, 512], BF16)
        mixT4 = sb("mixT4", [128, 4, 512], BF16)
        S_ring = [sb("S_ring%d" % i, [128, 4, 256], BF16) for i in range(5)]
        Mmat = sb("Mmat", [128, 16, 128], BF16)
        cf_sb = sb("cf_sb", [128, 4, 128], F32)
        gam_rep = sb("gam_rep", [128, 1024], F32)
        gf_rep = sb("gf_rep", [128, 1024], F32)
        g1_sb = sb("g1_sb", [128, 8], F32)
        g2_sb = sb("g2_sb", [128, 8], F32)
        ss = sb("ss", [128, 4], F32)
        rstd = sb("rstd", [128, 4], F32)
        ssq = sb("ssq", [128, 4], F32)
        rsq = sb("rsq", [128, 4], F32)
        elast = sb("elast", [128, 4, 2, 4], F32)
        neghalf = sb("neghalf", [128, 4], F32)

        v_sb = regX[:, 0:4096].rearrange("p (j c) -> p j c", j=4)
        Ga = regX[:, 4096:8192].rearrange("p (j c) -> p j c", j=4)
        Gb = regX[:, 8192:12288].rearrange("p (j c) -> p j c", j=4)
        uT = regX[:, 0:NFF * 512].rearrange("p (f n) -> p f n", f=NFF)
        Eb = regY[:, 0:2048].rearrange("p (h n) -> p h n", h=4)
        Enb = regY[:, 2048:4096].rearrange("p (h n) -> p h n", h=4)
        bufA = regY[:, 0:1024]
        bufB = regY[:, 1024:2048]
        merged = regY[:, 2048:2560].bitcast(BF16)
        mT_sb = regY[:, 2560:3072].bitcast(BF16).rearrange("p (k t) -> p k t", k=8)
        attn_sb = regY[:, 3072:3328].bitcast(BF16).rearrange("p (h t) -> p h t", h=4)
        ktok_sb = regY[:, 3328:3584].bitcast(BF16).rearrange("p (h t) -> p h t", h=4)
        mixT_sb = regY[:, 3584:3840].bitcast(BF16).rearrange("p (h t) -> p h t", h=4)
        merged2 = [merged, regY[:, 3072:3584].bitcast(BF16)]
        sg_t = regY[:, 0:512]
        p_f32 = ybuf_t[:]
        wp_stage = regY[:, 2048:3072].rearrange("p (g c) -> p g c", g=4)

        bank0 = ps("bank0", [128, 512], F32)
        bank1 = ps("bank1", [128, 512], F32)
        pr23 = ps("pr23", [128, 1024], F32)
        pr45 = ps("pr45", [128, 1024], F32)
        pr67 = ps("pr67", [128, 1024], F32)
        bank0_bf = bank0[:].bitcast(BF16)
        bank1_bf = bank1[:].bitcast(BF16)

        RB = [Res("bank%d" % i) for i in range(8)]
        R_slab = [Res("slab%d" % i) for i in range(3)]
        R_wsl = [Res("wsl%d" % i) for i in range(N_SLABS)]
        R_wslb = [Res("wslb%d" % i) for i in range(N_SLABS)]
        R_X = [Res("X%d" % i) for i in range(12)]
        R_Y = [Res("Y%d" % i) for i in range(16)]
        R = {}

        def r(name):
            if name not in R:
                R[name] = Res(name)
            return R[name]

        R_xh2 = [[Res("xh%d_%d" % (i, j)) for j in range(4)] for i in range(2)]
        R_xb = [Res("xb0"), Res("xb1")]
        R_ssA = [Res("ssA%d" % j) for j in range(4)]
        R_ss = [Res("ss%d" % j) for j in range(4)]
        R_junk = [Res("junk%d" % j) for j in range(4)]
        R_xnT = [Res("xnT%d" % j) for j in range(4)]
        R_qk = [Res("qk%d" % j) for j in range(4)]
        R_p = [Res("p%d" % j) for j in range(5)]
        R_S = [[Res("S%d_%d" % (i, h)) for h in range(4)] for i in range(3)]
        R_Sb = [Res("Sb%d" % i) for i in range(3)]
        R_ring = [[Res("ring%d_%d" % (i, h)) for h in range(4)] for i in range(5)]
        R_ssq = [Res("ssq%d" % h) for h in range(4)]
        R_attn = [Res("attn%d" % i) for i in range(4)]
        R_ktok = [Res("ktok%d" % i) for i in range(4)]
        R_mixT = [Res("mixT%d" % i) for i in range(4)]
        ring = [0]
        RY_Eb = R_Y[0:8]
        RY_Enb = R_Y[8:16]
        RY_bufA = R_Y[0:4]
        RY_bufB = R_Y[4:8]
        RY_merged = R_Y[8:10]
        RY_mT = R_Y[10:12]
        RY_attn = R_Y[12:13]
        RY_ktok = R_Y[13:14]
        RY_mixT = R_Y[14:15]
        RY_merged2 = [R_Y[8:10], R_Y[12:14]]
        RY_sg = R_Y[0:2]
        RY_ybuf = [r("ybuf")]

        def PE(fn, reads=(), writes=()):
            return P.add("pe", fn, reads, writes)

        def ACT(fn, reads=(), writes=()):
            return P.add("act", fn, reads, writes)

        def DVE(fn, reads=(), writes=()):
            return P.add("dve", fn, reads, writes)

        def POOL(fn, reads=(), writes=()):
            return P.add("pool", fn, reads, writes)

        dma_ctr = [0]

        def DMA(q, fn, reads=(), writes=(), key=None):
            if key is None:
                dma_ctr[0] += 1
                key = "u%d" % dma_ctr[0]
            return P.add(q, fn, reads, writes, dma_key=key)

        tiles = [[dict(kind="meta"), dict(kind="sample")]]
        for s in range(NB_SEQ):
            for tt in range(4):
                tiles.append([dict(kind="prompt", seq=s, blk=tt * 4 + j) for j in range(4)])
        if n_tiles_limit is not None:
            tiles = tiles[:n_tiles_limit]


        def load_x(ti):
            xh_t, R_xh_t = xh2[ti % 2], R_xh2[ti % 2]
            for j, b in enumerate(tiles[ti]):
                kx = "xh%d_%d" % (ti % 2, j)
                if b["kind"] == "meta":
                    POOL(I("memset", xh_t[:, j, :], 0.0), writes=[R_xh_t[j]])
                    DMA("pool", I("dma_start", out=xh_t[112:128, j, :], in_=meta), writes=[R_xh_t[j]], key=kx)
                elif b["kind"] == "sample":
                    DMA("pool", I("dma_start", out=xh_t[:, j, :], in_=xs), writes=[R_xh_t[j]], key=kx)
                else:
                    t0 = b["blk"] * 128
                    DMA("pool", I("dma_start", out=xh_t[:, j, :], in_=xp[b["seq"], t0:t0 + 128, :]),
                        writes=[R_xh_t[j]], key=kx)

        DMA("pool", I("dma_start", out=ident[:], in_=cst_b[:, 0:128]), writes=[r("ident")])
        DMA("pool", I("dma_start", out=Mmat[:], in_=cst_b[:, 128:17 * 128].rearrange("p (m t) -> p m t", m=16)), writes=[r("Mmat")])
        DMA("sp", I("dma_start", out=cf_sb[:], in_=cst_f.rearrange("p (m t) -> p m t", m=4)), writes=[r("cf")])
        DMA("sp", I("dma_start", out=g1_sb[:], in_=g1c), writes=[r("g1")])
        DMA("sp", I("dma_start", out=g2_sb[:], in_=g2c), writes=[r("g2")])
        DMA("sp", I("dma_start", out=gam_rep[:], in_=gla_g.partition_broadcast(128)), writes=[r("gam")])
        DMA("sp", I("dma_start", out=gf_rep[:], in_=gf.partition_broadcast(128)), writes=[r("gfr")])
        DMA("sp", I("dma_start", out=bufA, in_=pscale.partition_broadcast(128)), writes=RY_bufA)
        DMA("sp", I("dma_start", out=wp_stage, in_=w_pool.rearrange("g c d -> c g d")), writes=R_Y[8:12])
        DVE(I("tensor_tensor", out=wpool_sb[:], in0=wp_stage, in1=bufA.rearrange("p (g c) -> p g c", g=4), op=ALU.mult),
            reads=RY_bufA + R_Y[8:12], writes=[r("wpool")])
        DMA("pool", I("dma_start", out=wa_sb[:], in_=w_in[:, C_A:C_A + 16].rearrange("(k p) c -> p k c", p=128)), writes=[r("wa")])
        POOL(I("memset", w2aug[:], 0.0), writes=[r("w2aug")])
        POOL(I("memset", neghalf[:], -0.5), writes=[r("neghalf")])
        DMA("pool", I("dma_start", out=w2aug[0:16, :], in_=w_a2), writes=[r("w2aug")])
        DMA("pool", I("dma_start", out=w2aug[32:33, :], in_=b_al), writes=[r("w2aug")])
        POOL(I("memset", a_aug[:], 0.0), writes=[r("a_aug")])
        POOL(I("memset", a_aug[32:33, :], 1.0), writes=[r("a_aug")])
        POOL(I("memset", p_hist[:], 0.0), writes=[r("p_hist")])
        DMA("pool", I("dma_start", out=p_hist[113:128, :], in_=sp_in[0]), writes=[r("p_hist")])
        DMA("pool", I("dma_start", out=p_hist[49:64, :], in_=sp_in[1]), writes=[r("p_hist")])

        def cast_slab(i):
            if i < 11:
                c0 = IN_SLABS[i]
                DMA("pool", I("dma_start", out=wsl[i], in_=w_in[:, c0:c0 + 512].rearrange("(k p) c -> p k c", p=128)),
                    writes=[R_wsl[i]])
            elif i < 22:
                jj = i - 11
                DMA("pool", I("dma_start", out=wsl[i][:, :, 0:256], in_=w_fg[:, jj * 256:(jj + 1) * 256].rearrange("(k p) c -> p k c", p=128)),
                    writes=[R_wsl[i]])
                DMA("pool", I("dma_start", out=wsl[i][:, :, 256:512], in_=w_fu[:, jj * 256:(jj + 1) * 256].rearrange("(k p) c -> p k c", p=128)),
                    writes=[R_wslb[i]])
            else:
                half, jj = divmod(i - 22, 3)
                n = 8 if jj < 2 else 6
                src = w_fd.rearrange("(f p) c -> p f c", p=128)[:, jj * 8:jj * 8 + n, half * 512:(half + 1) * 512]
                DMA("pool", I("dma_start", out=wsl[i][:, 0:n, :], in_=src), writes=[R_wsl[i]])

        POOL(I("memset", S_f[1][:], 0.0), writes=R_S[1])
        POOL(I("memset", S_ring[0][:], 0.0), writes=R_ring[0])
        DMA("sp", I("dma_start", out=S_f[0][:], in_=sg_in[0].rearrange("h k v -> k h v")), writes=R_S[0], key="S0")
        DMA("sp", I("dma_start", out=S_f[2][:], in_=sg_in[1].rearrange("h k v -> k h v")), writes=R_S[2], key="S2")
        ACT(I("activation", out=S_b[0][:], in_=S_f[0][:], func=AF.Copy), reads=R_S[0], writes=[R_Sb[0]])
        ACT(I("activation", out=S_b[2][:], in_=S_f[2][:], func=AF.Copy), reads=R_S[2], writes=[R_Sb[2]])
        load_x(0)
        for i in range(11):
            cast_slab(i)
        for kc in range(8):
            DMA("pool", I("dma_start", out=wout_sb[:, kc, :], in_=w_out[kc * 128:(kc + 1) * 128, :]), writes=[r("wout%d" % kc)])
        cast_done = [11]

        def cast_upto(k):
            while cast_done[0] <= k and cast_done[0] < N_SLABS:
                cast_slab(cast_done[0])
                cast_done[0] += 1


        slab_seq = [0]
        loaded = [0]
        total_slabs = len(tiles) * N_SLABS

        def ensure(idx):
            while loaded[0] <= idx and loaded[0] < total_slabs:
                i = loaded[0]
                sid = i % N_SLABS
                b = i % 3
                nv = 6 if sid in (24, 27) else 8
                DMA("sp", I("dma_start", out=slab[b][:, 0:nv, :], in_=wsl[sid][:, 0:nv, :]),
                    reads=[R_wsl[sid], R_wslb[sid]], writes=[R_slab[b]], key="slab%d" % b)
                loaded[0] += 1

        def next_slab(prefetch=2):
            idx = slab_seq[0]
            if idx < N_SLABS:
                cast_upto(idx + 4)
            ensure(idx + prefetch)
            slab_seq[0] += 1
            b = idx % 3
            return slab[b], R_slab[b]

        psum_tok = [(bank0[:], [RB[0]]), (bank1[:], [RB[1]]), (pr23[:, 0:512], [RB[2]]), (pr23[:, 512:1024], [RB[3]])]
        tok_ctr = [0]

        def next_tok_psum():
            pr = psum_tok[tok_ctr[0] % 4]
            tok_ctr[0] += 1
            return pr

        def rms_rstd(src_ss, dst, c0, c1, scale, rs, ws):
            ACT(I("activation", out=dst[:, c0:c1], in_=src_ss[:, c0:c1], func=AF.Ln, scale=scale, bias=EPS), reads=rs, writes=ws)
            ACT(I("activation", out=dst[:, c0:c1], in_=dst[:, c0:c1], func=AF.Exp, scale=-0.5), reads=ws, writes=ws)

        def rms_rstd_pow(src_ss, dst, c0, c1, scale, rs, ws):
            DVE(I("tensor_scalar", out=dst[:, c0:c1], in0=src_ss[:, c0:c1], scalar1=scale, scalar2=EPS, op0=ALU.mult, op1=ALU.add), reads=rs, writes=ws)
            POOL(I("tensor_tensor", out=dst[:, c0:c1], in0=dst[:, c0:c1], in1=neghalf[:, 0:c1 - c0], op=ALU.pow), reads=ws + [r("neghalf")], writes=ws)

        def nt_act(xh_t, R_xh_t, j, rstd_t, rres, par):
            ACT(I("activation", out=xb2[par][:], in_=xh_t[:, j, :], func=AF.Copy, scale=rstd_t[:, j:j + 1]),
                reads=[R_xh_t[j]] + rres, writes=[R_xb[par]])

        def nt_pe(j, gsb, gres, par):
            pT = bank0_bf.rearrange("p (k t) -> p k t", k=8)
            for k in range(8):
                PE(I("transpose", out=pT[:, k, :], in_=xb2[par][:, k * 128:(k + 1) * 128], identity=ident[:]),
                   reads=[R_xb[par], r("ident")], writes=[RB[0]])
            DVE(I("tensor_tensor", out=xnT[:, :, j * 128:(j + 1) * 128], in0=pT,
                  in1=gsb[:].unsqueeze(2).to_broadcast([128, 8, 128]), op=ALU.mult),
                reads=[RB[0], gres], writes=[R_xnT[j]])

        def phase_AB12(ti):
            blocks = tiles[ti]
            nb = len(blocks)
            N = nb * 128
            xh, R_xh = xh2[ti % 2], R_xh2[ti % 2]
            for j in range(nb):
                ACT(I("activation", out=junk[:], in_=xh[:, j, :], func=AF.Square, accum_out=ssA[:, j:j + 1]),
                    reads=[R_xh[j]], writes=[R_ssA[j]] + R_junk)
                yield "early"
            rms_rstd(ssA, rstdA, 0, nb, 1.0 / D, R_ssA[0:nb], [r("rstdA")])
            yield "early"
            nt_act(xh, R_xh, 0, rstdA, [r("rstdA")], 0)
            yield "early_last"
            for j in range(nb):
                if j + 1 < nb:
                    nt_act(xh, R_xh, j + 1, rstdA, [r("rstdA")], (j + 1) % 2)
                nt_pe(j, g1_sb, r("g1"), j % 2)
                yield
            pA = pr23[0:16, 0:N]
            for kc in range(8):
                PE(I("matmul", pA, lhsT=wa_sb[:, kc, :], rhs=xnT[:, kc, 0:N], start=(kc == 0), stop=(kc == 7)),
                   reads=[r("wa")] + R_xnT[0:nb], writes=[RB[2]])
            ACT(I("activation", out=a_aug[0:16, 0:N], in_=pA, func=AF.Copy), reads=[RB[2]], writes=[r("a_aug")])
            yield
            for j, b in enumerate(blocks):
                smp = b["kind"] == "sample"
                tri = cf_sb[:, 3, :] if smp else cf_sb[:, 2, :]
                pZ = pr23[:, 512:1024]
                PE(I("matmul", pZ, lhsT=a_aug[0:33, j * 128:(j + 1) * 128], rhs=w2aug[0:33, :], start=True, stop=True),
                   reads=[r("a_aug"), r("w2aug")], writes=[RB[3]])
                ACT(I("activation", out=gt[:], in_=pZ, func=AF.Exp, scale=-1.0), reads=[RB[3]], writes=[r("gt")])
                ACT(I("activation", out=gt[:], in_=gt[:], func=AF.Ln, bias=1.0), reads=[r("gt")], writes=[r("gt")])
                yield
                pBk = bank1[:].rearrange("p (h t) -> p h t", h=4)
                for h in range(4):
                    PE(I("matmul", pBk[:, h, :], lhsT=gt[:, h * 128:(h + 1) * 128], rhs=tri, start=True, stop=True),
                       reads=[r("gt"), r("cf")], writes=[RB[1]])
                ACT(I("activation", out=Eb[:, :, j * 128:(j + 1) * 128], in_=pBk, func=AF.Exp),
                    reads=[RB[1]], writes=RY_Eb)
                ACT(I("activation", out=Enb[:, :, j * 128:(j + 1) * 128], in_=pBk, func=AF.Exp, scale=-1.0),
                    reads=[RB[1]], writes=RY_Enb)
                DVE(I("tensor_copy", out=elast[:, j, 1, :], in_=Eb[:, :, j * 128 + 127]), reads=RY_Eb, writes=[r("elast")])
                if smp:
                    DVE(I("tensor_copy", out=elast[:, j, 0, :], in_=Eb[:, :, j * 128 + 63]), reads=RY_Eb, writes=[r("elast")])
                yield

        for _ in phase_AB12(0):
            pass

        pending_d4 = []
        for ti, blocks in enumerate(tiles):
            nb = len(blocks)
            N = nb * 128
            xh, R_xh = xh2[ti % 2], R_xh2[ti % 2]
            first_of_seq = blocks[0]["kind"] == "prompt" and blocks[0]["blk"] == 0
            if ti == 1 or first_of_seq:
                if ti >= 1:
                    POOL(I("tensor_copy", out=S_f[0][:], in_=S_f[1][:]), reads=R_S[1], writes=R_S[0])
                    POOL(I("tensor_copy", out=S_ring[ring[0] % 5][:], in_=S_b[1][:]), reads=[R_Sb[1]], writes=R_ring[ring[0] % 5])
                    POOL(I("tensor_copy", out=p_sb[:, 0, :], in_=p_meta[:]), reads=[r("p_meta")], writes=[R_p[0]])

            for which in range(2):
                sl, rsl = next_slab()
                dst = qTt if which == 0 else kTt
                E = Eb if which == 0 else Enb
                RE = RY_Eb if which == 0 else RY_Enb
                sc = QSCALE if which == 0 else 1.0
                for h in range(4):
                    pq, rpq = next_tok_psum()
                    for kc in range(8):
                        PE(I("matmul", pq[:, 0:N], lhsT=sl[:, kc, h * 128:(h + 1) * 128], rhs=xnT[:, kc, 0:N],
                                                                        start=(kc == 0), stop=(kc == 7)),
                           reads=[rsl] + R_xnT[0:nb], writes=rpq)
                    DVE(I("scalar_tensor_tensor", out=dst[:, h, 0:N], in0=pq[:, 0:N], scalar=sc, in1=E[:, h, 0:N],
                                                                                           op0=ALU.mult, op1=ALU.mult),
                        reads=rpq + RE, writes=R_qk[0:nb])

            for fn in pending_d4:
                fn()
            del pending_d4[:]

            for c in range(2):
                sl, rsl = next_slab()
                for j in range(nb):
                    pv, rpv = next_tok_psum()
                    for kc in range(8):
                        PE(I("matmul", pv, lhsT=xnT[:, kc, j * 128:(j + 1) * 128], rhs=sl[:, kc, :],
                                                                        start=(kc == 0), stop=(kc == 7)),
                           reads=[rsl, R_xnT[j]], writes=rpv)
                    ACT(I("activation", out=v_sb[:, j, c * 512:(c + 1) * 512], in_=pv, func=AF.Copy),
                        reads=rpv, writes=[R_X[j]])
            sl, rsl = next_slab()
            for j, b in enumerate(blocks):
                pv, rpv = next_tok_psum()
                for kc in range(8):
                    PE(I("matmul", pv, lhsT=xnT[:, kc, j * 128:(j + 1) * 128], rhs=sl[:, kc, :],
                                                                    start=(kc == 0), stop=(kc == 7)),
                       reads=[rsl, R_xnT[j]], writes=rpv)
                DVE(I("tensor_copy", out=p_sb[:, j + 1, :], in_=pv), reads=rpv, writes=[R_p[j + 1]])
                last_blk = (b["kind"] == "sample") or (b["kind"] == "prompt" and b["blk"] == 15)
                if b["kind"] == "meta":
                    POOL(I("tensor_copy", out=p_meta[:], in_=p_sb[:, j + 1, :]), reads=[R_p[j + 1]], writes=[r("p_meta")])
                if last_blk:
                    DVE(I("tensor_copy", out=p_f32, in_=pv), reads=rpv, writes=[r("ybuf")])
                    if b["kind"] == "sample":
                        DMA("pool", I("dma_start", out=sps[0], in_=p_f32[49:64, :]), reads=[r("ybuf")], key="pf")
                        DMA("pool", I("dma_start", out=sps[1], in_=p_f32[113:128, :]), reads=[r("ybuf")], key="pf")
                    else:
                        DMA("pool", I("dma_start", out=spp[b["seq"]], in_=p_f32[113:128, :]), reads=[r("ybuf")], key="pf")

            if ti + 1 < len(tiles):
                load_x(ti + 1)

            blk_info = []
            for j, b in enumerate(blocks):
                smp = b["kind"] == "sample"
                js = slice(j * 128, (j + 1) * 128)
                mask = cf_sb[:, 1, :] if smp else cf_sb[:, 0, :]
                if b["kind"] == "meta" or smp:
                    pprev, rpprev = p_hist[:], r("p_hist")
                else:
                    pprev, rpprev = p_sb[:, j, :], R_p[j]
                mc0 = 8 if smp else 0
                mp0 = 12 if smp else 4
                par = j % 2
                pAt = (bank0 if par == 0 else bank1)[:].rearrange("p (h t) -> p h t", h=4)
                rAt = [RB[par]]
                a4 = attn4[:, j, :].rearrange("p (h t) -> p h t", h=4)
                for h in range(4):
                    PE(I("matmul", pAt[:, h, :], lhsT=kTt[:, h, js], rhs=qTt[:, h, js], start=True, stop=True),
                       reads=[R_qk[j]], writes=rAt)
                DVE(I("tensor_tensor", out=a4, in0=pAt, in1=mask.unsqueeze(1).to_broadcast([128, 4, 128]), op=ALU.mult),
                    reads=rAt + [r("cf")], writes=[R_attn[j]])
                pKt = pr23[:, par * 512:(par + 1) * 512].bitcast(BF16)[:, 0:512].rearrange("p (h t) -> p h t", h=4)
                rKt = [RB[2 + par]]
                k4 = ktok4[:, j, :].rearrange("p (h t) -> p h t", h=4)
                for h in range(4):
                    PE(I("transpose", out=pKt[:, h, :], in_=kTt[:, h, js], identity=ident[:]),
                       reads=[R_qk[j], r("ident")], writes=rKt)
                ACT(I("activation", out=k4, in_=pKt, func=AF.Copy), reads=rKt, writes=[R_ktok[j]])
                pMix = pr45[:, par * 512:(par + 1) * 512].rearrange("p (g t) -> p g t", g=4)
                rMix = [RB[4 + par]]
                m4 = mixT4[:, j, :].rearrange("p (g t) -> p g t", g=4)
                for g in range(4):
                    gs = slice(g * 128, (g + 1) * 128)
                    PE(I("matmul", pMix[:, g, :], lhsT=pprev[:, gs], rhs=Mmat[:, mp0 + g, :], start=True, stop=False),
                       reads=[rpprev, r("Mmat")], writes=rMix)
                    PE(I("matmul", pMix[:, g, :], lhsT=p_sb[:, j + 1, gs], rhs=Mmat[:, mc0 + g, :], start=False, stop=True),
                       reads=[R_p[j + 1], r("Mmat")], writes=rMix)
                ACT(I("activation", out=m4, in_=pMix, func=AF.Copy), reads=rMix, writes=[R_mixT[j]])
                blk_info.append(dict(a4=a4, k4=k4, m4=m4, smp=smp))

            def state_step(j, b):
                smp = b["kind"] == "sample"
                k4 = blk_info[j]["k4"]
                if smp:
                    POOL(I("tensor_copy", out=S_b[1][:], in_=S_ring[ring[0] % 5][:]), reads=R_ring[ring[0] % 5], writes=[R_Sb[1]])
                    sts = [(0, slice(0, 64), 0), (2, slice(64, 128), 1)]
                    blk_info[j]["S_in"] = [(S_b[0], [R_Sb[0]] * 4, slice(0, 64)), (S_b[2], [R_Sb[2]] * 4, slice(64, 128))]
                else:
                    sts = [(1 if b["kind"] == "meta" else 0, slice(0, 128), 1)]
                    blk_info[j]["S_in"] = [(S_ring[ring[0] % 5], R_ring[ring[0] % 5], slice(0, 128))]
                pS = pr67[:].rearrange("p (h v) -> p h v", h=4)
                rbs = [RB[6], RB[7]]
                for (si, psl, wh) in sts:
                    for h in range(4):
                        hs = slice(h * 256, (h + 1) * 256)
                        PE(I("matmul", pS[:, h, :], lhsT=k4[psl, h, :], rhs=v_sb[psl, j, hs], start=True, stop=True),
                           reads=[R_ktok[j], R_X[j]], writes=[rbs[h // 2]])
                    DVE(I("tensor_tensor", out=S_f[si][:], in0=S_f[si][:], in1=pS, op=ALU.add),
                        reads=rbs + R_S[si], writes=R_S[si])
                    if not smp:
                        so = (ring[0] + 1) % 5
                        for h in range(4):
                            ACT(I("activation", out=S_ring[so][:, h, :], in_=S_f[si][:, h, :], func=AF.Copy, scale=elast[:, j, wh, h:h + 1]),
                                reads=[R_S[si][h], r("elast")], writes=[R_ring[so][h]])
                    for h in range(4):
                        DVE(I("tensor_scalar", out=S_f[si][:, h, :], in0=S_f[si][:, h, :], scalar1=elast[:, j, wh, h:h + 1], scalar2=None, op0=ALU.mult),
                            reads=[R_S[si][h], r("elast")], writes=[R_S[si][h]])
                if not smp:
                    ring[0] += 1
                if smp:
                    DMA("pool", I("dma_start", out=sgs[0].rearrange("h k v -> k h v"), in_=S_f[0][:]), reads=R_S[0], key="So0")
                    DMA("pool", I("dma_start", out=sgs[1].rearrange("h k v -> k h v"), in_=S_f[2][:]), reads=R_S[2], key="So2")
                elif b["kind"] == "prompt" and b["blk"] == 15:
                    DMA("pool", I("dma_start", out=sgp[b["seq"]].rearrange("h k v -> k h v"), in_=S_f[0][:]), reads=R_S[0], key="So0")

            slab_i = 0
            hbs = [(bufA[:, 0:512], RY_bufA[0:2]), (bufA[:, 512:1024], RY_bufA[2:4]),
                   (bufB[:, 0:512], RY_bufB[0:2]), (bufB[:, 512:1024], RY_bufB[2:4])]
            hbs16 = [(hb_[:, 0:256].bitcast(BF16), rr_) for (hb_, rr_) in hbs]
            hctr = 0
            for kind in range(2):
                for c in range(2):
                    sl, rsl = next_slab()
                    for j in range(nb):
                        pv, rpv = next_tok_psum()
                        for kc in range(8):
                            PE(I("matmul", pv, lhsT=xnT[:, kc, j * 128:(j + 1) * 128], rhs=sl[:, kc, :],
                                 start=(kc == 0), stop=(kc == 7)),
                               reads=[rsl, R_xnT[j]], writes=rpv)
                        cs = slice(c * 512, (c + 1) * 512)
                        hb, rhb = hbs[hctr % 4]
                        hctr += 1
                        if kind == 0:
                            ACT(I("activation", out=hb, in_=pv, func=AF.Sigmoid), reads=rpv, writes=rhb)
                            DVE(I("tensor_tensor", out=Ga[:, j, cs], in0=pv, in1=hb, op=ALU.mult),
                                reads=rpv + rhb, writes=[R_X[4 + j]])
                            POOL(I("tensor_tensor", out=Ga[:, j, cs], in0=Ga[:, j, cs], in1=gam_rep[:, cs], op=ALU.mult),
                                 reads=[R_X[4 + j], r("gam")], writes=[R_X[4 + j]])
                        else:
                            ACT(I("activation", out=hb, in_=pv, func=AF.Sigmoid), reads=rpv, writes=rhb)
                            DVE(I("tensor_tensor", out=Ga[:, j, cs], in0=Ga[:, j, cs], in1=hb, op=ALU.mult),
                                reads=rhb + [R_X[4 + j]], writes=[R_X[4 + j]])
                    if slab_i < nb and slab_i != 3:
                        state_step(slab_i, blocks[slab_i])
                    slab_i += 1
            gb_slabs = [next_slab(), next_slab(prefetch=1)]

            def gb_part(j, c, bk=0):
                sl, rsl = gb_slabs[c]
                pb = {0: bank0[:], 1: bank1[:], 7: pr67[:, 512:1024]}[bk]
                for kc in range(8):
                    PE(I("matmul", pb, lhsT=xnT[:, kc, j * 128:(j + 1) * 128], rhs=sl[:, kc, :],
                         start=(kc == 0), stop=(kc == 7)),
                       reads=[rsl, R_xnT[j]], writes=[RB[bk]])
                ACT(I("activation", out=Gb[:, j, c * 512:(c + 1) * 512], in_=pb, func=AF.Sigmoid),
                    reads=[RB[bk]], writes=[R_X[8 + j]])

            bufA_bf = regY[:, 0:512].bitcast(BF16)
            bufB_bf = regY[:, 1024:1536].bitcast(BF16)
            RY_Abf = R_Y[0:2]
            RY_Bbf = R_Y[4:6]

            def op_T(j):
                mg = merged2[j % 2]
                rmg = RY_merged2[j % 2]
                pMt = bank1_bf.rearrange("p (k t) -> p k t", k=8)
                for k in range(8):
                    PE(I("transpose", out=pMt[:, k, :], in_=mg[:, k * 128:(k + 1) * 128], identity=ident[:]),
                       reads=rmg + [r("ident")], writes=[RB[1]])
                ACT(I("activation", out=mT_sb, in_=pMt, func=AF.Copy), reads=[RB[1]], writes=RY_mT)

            def op_mm(j, halves=(0, 1)):
                pOut = pr67[:]
                for c in halves:
                    for kc in range(8):
                        PE(I("matmul", pOut[:, c * 512:(c + 1) * 512], lhsT=mT_sb[:, kc, :], rhs=wout_sb[:, kc, c * 512:(c + 1) * 512],
                             start=(kc == 0), stop=(kc == 7)),
                           reads=RY_mT + [r("wout%d" % kc)], writes=[RB[6 + c]])
                if 1 in halves:
                    DVE(I("tensor_tensor", out=xh[:, j, :], in0=xh[:, j, :], in1=pOut, op=ALU.add),
                        reads=[RB[6], RB[7], R_xh[j]], writes=[R_xh[j]])

            def d1_act(j):
                ACT(I("activation", out=junk[:], in_=xh[:, j, :], func=AF.Square, accum_out=ss[:, j:j + 1]),
                    reads=[R_xh[j]], writes=[R_ss[j]] + R_junk)
                rms_rstd_pow(ss, rstd, j, j + 1, 1.0 / D, [R_ss[j]], [R_ss[j]])
                nt_act(xh, R_xh, j, rstd, [R_ss[j]], j % 2)

            gb_part(0, 0)
            gb_part(0, 1, bk=1)
            if nb == 4:
                state_step(3, blocks[3])
            for j, b in enumerate(blocks):
                bi = blk_info[j]
                pO = pr23[:]
                for h in range(4):
                    hs = slice(h * 256, (h + 1) * 256)
                    PE(I("matmul", pO[:, hs], lhsT=bi["a4"][:, h, :], rhs=v_sb[:, j, hs], start=True, stop=False),
                       reads=[R_attn[j], R_X[j]], writes=[RB[2 + h // 2]])
                    for (Sb_t, Sb_r, psl) in bi["S_in"]:
                        PE(I("matmul", pO[psl, hs], lhsT=qTt[:, h, j * 128 + psl.start:j * 128 + psl.stop], rhs=Sb_t[:, h, :], start=False, stop=True),
                           reads=[R_qk[j], Sb_r[h]], writes=[RB[2 + h // 2]])
                pYb = pr45[:]
                rYb = [RB[4], RB[5]]
                for g in range(4):
                    PE(I("matmul", pYb[:, g * 256:(g + 1) * 256], lhsT=bi["m4"][:, g, :], rhs=wpool_sb[:, g, :], start=True, stop=True),
                       reads=[R_mixT[j], r("wpool")], writes=[rYb[g // 2]])
                DVE(I("tensor_tensor", out=bufB_bf, in0=pYb, in1=Gb[:, j, :], op=ALU.mult),
                    reads=rYb + [R_X[8 + j]], writes=RY_Bbf)
                fill = [q for q in {0: [1, 2], 1: [3]}.get(j, []) if q < nb]
                early_T = (j >= 1 and not fill)
                if early_T:
                    op_T(j - 1)
                for h in range(4):
                    hs = slice(h * 256, (h + 1) * 256)
                    ACT(I("activation", out=junk[:, hs], in_=pO[:, hs], func=AF.Square, accum_out=ssq[:, h:h + 1]),
                        reads=[RB[2 + h // 2]], writes=[R_ssq[h], R_junk[h]])
                rms_rstd_pow(ssq, rsq, 0, 4, 1.0 / 256, R_ssq, [r("rsq")])
                for h in range(4):
                    hs = slice(h * 256, (h + 1) * 256)
                    DVE(I("scalar_tensor_tensor", out=bufA_bf[:, hs], in0=pO[:, hs], scalar=rsq[:, h:h + 1], in1=Ga[:, j, hs],
                          op0=ALU.mult, op1=ALU.mult),
                        reads=[RB[2 + h // 2], r("rsq"), R_X[4 + j]], writes=[RY_Abf[h // 2]])
                DVE(I("tensor_tensor", out=merged2[j % 2], in0=bufA_bf, in1=bufB_bf, op=ALU.add), reads=RY_Abf + RY_Bbf, writes=RY_merged2[j % 2])
                if fill:
                    gb_part(fill[0], 0)
                if j >= 1 and not early_T:
                    op_T(j - 1)
                if fill:
                    gb_part(fill[0], 1, bk=7)
                for q in fill[1:]:
                    gb_part(q, 0)
                    gb_part(q, 1, bk=7)
                if j >= 1:
                    op_mm(j - 1)
                if j >= 2:
                    d1_act(j - 2)
                if j >= 3:
                    nt_pe(j - 3, g2_sb, r("g2"), (j - 3) % 2)
            op_T(nb - 1)
            if nb >= 2:
                d1_act(nb - 2)
            if nb >= 3:
                nt_pe(nb - 3, g2_sb, r("g2"), (nb - 3) % 2)
            op_mm(nb - 1, halves=(0,))
            if nb >= 2:
                nt_pe(nb - 2, g2_sb, r("g2"), (nb - 2) % 2)
            op_mm(nb - 1, halves=(1,))
            d1_act(nb - 1)
            N1 = N - 128
            sl0, rsl0 = next_slab()
            ebanks = [(bank1[:], [RB[1]]), (pr23[:, 0:512], [RB[2]]), (pr23[:, 512:1024], [RB[3]]), (pr45[:, 0:512], [RB[4]])]

            def gu_part(c0, c1, rres):
                for cc in range(2):
                    for which in range(2):
                        pb, rb = ebanks[2 * cc + which]
                        w0 = which * 256 + cc * 128
                        for kc in range(8):
                            PE(I("matmul", pb[:, c0:c1], lhsT=sl0[:, kc, w0:w0 + 128], rhs=xnT[:, kc, c0:c1],
                                 start=(kc == 0), stop=(kc == 7)),
                               reads=[rsl0] + rres, writes=rb)

            if N1 > 0:
                gu_part(0, N1, R_xnT[0:nb - 1])
            nt_pe(nb - 1, g2_sb, r("g2"), (nb - 1) % 2)
            gu_part(N1, N, [R_xnT[nb - 1]])
            for cc in range(2):
                pG, rG = ebanks[2 * cc]
                pU, rU = ebanks[2 * cc + 1]
                ACT(I("activation", out=sg_t[:, 0:N], in_=pG[:, 0:N], func=AF.Sigmoid), reads=rG, writes=RY_sg)
                DVE(I("tensor_tensor", out=sg_t[:, 0:N], in0=pG[:, 0:N], in1=sg_t[:, 0:N], op=ALU.mult), reads=rG + RY_sg, writes=RY_sg)
                DVE(I("tensor_tensor", out=uT[:, cc, 0:N], in0=pU[:, 0:N], in1=sg_t[:, 0:N], op=ALU.mult),
                    reads=rU + RY_sg, writes=[R_X[0]])

            if blocks[-1]["kind"] == "prompt" and blocks[-1]["blk"] != 15:
                POOL(I("tensor_copy", out=p_sb[:, 0, :], in_=p_sb[:, nb, :]), reads=[R_p[nb]], writes=[R_p[0]])

            gu = [(bank0[:], [RB[0]]), (bank1[:], [RB[1]]), (pr23[:, 0:512], [RB[2]]), (pr23[:, 512:1024], [RB[3]])]
            gctr = 0
            hoist = None
            early_done = True
            for jj in range(1, 11):
                sl, rsl = next_slab()
                if jj == 7 and ti + 1 < len(tiles):
                    hoist = phase_AB12(ti + 1)
                    early_done = False
                for cc in range(2):
                    ffc = jj * 2 + cc
                    if hoist is not None and not early_done:
                        if next(hoist, None) == "early_last":
                            early_done = True
                    pG, rG = gu[gctr % 4]
                    pU, rU = gu[(gctr + 1) % 4]
                    gctr += 2
                    for kc in range(8):
                        PE(I("matmul", pG[:, 0:N], lhsT=sl[:, kc, cc * 128:(cc + 1) * 128], rhs=xnT[:, kc, 0:N],
                                                                          start=(kc == 0), stop=(kc == 7)),
                           reads=[rsl] + R_xnT[0:nb], writes=rG)
                    for kc in range(8):
                        PE(I("matmul", pU[:, 0:N], lhsT=sl[:, kc, 256 + cc * 128:256 + (cc + 1) * 128], rhs=xnT[:, kc, 0:N],
                                                                          start=(kc == 0), stop=(kc == 7)),
                           reads=[rsl] + R_xnT[0:nb], writes=rU)
                    ACT(I("activation", out=sg_t[:, 0:N], in_=pG[:, 0:N], func=AF.Sigmoid), reads=rG, writes=RY_sg)
                    DVE(I("tensor_tensor", out=sg_t[:, 0:N], in0=pG[:, 0:N], in1=sg_t[:, 0:N], op=ALU.mult), reads=rG + RY_sg, writes=RY_sg)
                    DVE(I("tensor_tensor", out=uT[:, ffc, 0:N], in0=pU[:, 0:N], in1=sg_t[:, 0:N], op=ALU.mult),
                        reads=rU + RY_sg, writes=[R_X[ffc // 2]])
            while hoist is not None and not early_done:
                if next(hoist, None) == "early_last":
                    early_done = True
            dacc = [(pr45[:, 0:512], [RB[4]]), (pr45[:, 512:1024], [RB[5]]), (pr67[:, 0:512], [RB[6]]), (pr67[:, 512:1024], [RB[7]])]
            for half in range(2):
                for jj in range(3):
                    sl, rsl = next_slab()
                    n = 8 if jj < 2 else 6
                    for i in range(n):
                        ffc = jj * 8 + i
                        if hoist is not None and i % 2 == 1:
                            next(hoist, None)
                        for j in range(nb):
                            PE(I("matmul", dacc[j][0], lhsT=uT[:, ffc, j * 128:(j + 1) * 128], rhs=sl[:, i, :],
                                                                            start=(ffc == 0), stop=(ffc == NFF - 1)),
                               reads=[rsl, R_X[ffc // 2]], writes=dacc[j][1])
                for j in range(nb):
                    hsl = slice(half * 512, (half + 1) * 512)
                    DVE(I("tensor_tensor", out=xh[:, j, hsl], in0=xh[:, j, hsl], in1=dacc[j][0], op=ALU.add),
                        reads=dacc[j][1] + [R_xh[j]], writes=[R_xh[j]])
            if hoist is not None:
                for _ in hoist:
                    pass
            def final_norm(ti=ti, blocks=blocks, xh=xh, R_xh=R_xh):
                for j, b in enumerate(blocks):
                    if b["kind"] == "meta":
                        continue
                    ACT(I("activation", out=junk[:], in_=xh[:, j, :], func=AF.Square, accum_out=ss[:, j:j + 1]),
                        reads=[R_xh[j]], writes=[R_ss[j]] + R_junk)
                    rms_rstd(ss, rstd, j, j + 1, 1.0 / D, [R_ss[j]], [R_ss[j]])
                    DVE(I("scalar_tensor_tensor", out=xh[:, j, :], in0=xh[:, j, :], scalar=rstd[:, j:j + 1], in1=gf_rep[:], op0=ALU.mult, op1=ALU.mult),
                        reads=[R_xh[j], R_ss[j], r("gfr")], writes=[R_xh[j]])
                    if b["kind"] == "sample":
                        DMA("pool", I("dma_start", out=ys, in_=xh[:, j, :]), reads=[R_xh[j]], key="yout%d_%d" % (ti % 2, j))
                    else:
                        t0 = b["blk"] * 128
                        DMA("pool", I("dma_start", out=yp[b["seq"], t0:t0 + 128, :], in_=xh[:, j, :]), reads=[R_xh[j]], key="yout%d_%d" % (ti % 2, j))

            pending_d4.append(final_norm)

        for fn in pending_d4:
            fn()
        del pending_d4[:]

        P.emit(st)
    return nc


_NC_CACHE = {}


def _get_nc(limit=None):
    if limit not in _NC_CACHE:
        _NC_CACHE[limit] = build_program(limit)
    return _NC_CACHE[limit]


def kernel(x_prompt, x_sample, state_gla, state_pool, meta_tokens, norm1_g, w_in, w_alpha2, b_alpha,
           gla_norm_g, w_pool, pool_scale, w_out, norm2_g, w_ffn_gate, w_ffn_up, w_ffn_down, norm_f_g,
           _limit=None):
    f = lambda a: np.ascontiguousarray(np.asarray(a, dtype=np.float32))
    cf, cb = _consts()
    nc = _get_nc(_limit)
    shared = {
        "meta": f(meta_tokens),
        "g1c": f(np.asarray(norm1_g).reshape(8, 128).T),
        "g2c": f(np.asarray(norm2_g).reshape(8, 128).T),
        "w_in": f(w_in), "w_a2": f(w_alpha2), "b_al": f(np.asarray(b_alpha).reshape(1, 512)),
        "gla_g": f(gla_norm_g), "w_pool": f(w_pool), "pscale": f(pool_scale), "w_out": f(w_out),
        "w_fg": f(w_ffn_gate), "w_fu": f(w_ffn_up), "w_fd": f(w_ffn_down), "gf": f(norm_f_g),
        "cst_f": cf, "cst_b": cb,
    }
    x_prompt = np.asarray(x_prompt)
    x_sample = np.asarray(x_sample)
    state_gla = np.asarray(state_gla)
    state_pool = np.asarray(state_pool)
    in_maps = []
    for c in range(8):
        m = dict(shared)
        m["xp"] = f(x_prompt[4 * c:4 * c + 4])
        m["xs"] = f(x_sample[2 * c:2 * c + 2].reshape(128, D))
        m["sg_in"] = f(state_gla[2 * c:2 * c + 2])
        m["sp_in"] = f(state_pool[2 * c:2 * c + 2])
        in_maps.append(m)
    res = run_bass_kernel_spmd(nc, in_maps, core_ids=list(range(8)))
    rs = res.results
    y_prompt = np.concatenate([rs[c]["yp"] for c in range(8)], axis=0).astype(np.float32)
    y_sample = np.concatenate([rs[c]["ys"].reshape(2, 64, D) for c in range(8)], axis=0).astype(np.float32)
    sgp = np.concatenate([rs[c]["sgp"] for c in range(8)], axis=0).astype(np.float32)
    spp = np.concatenate([rs[c]["spp"] for c in range(8)], axis=0).astype(np.float32)
    sgs = np.concatenate([rs[c]["sgs"] for c in range(8)], axis=0).astype(np.float32)
    sps = np.concatenate([rs[c]["sps"] for c in range(8)], axis=0).astype(np.float32)
    return (y_prompt, y_sample, sgp, spp, sgs, sps)
```

```python
import os
import numpy as np
from contextlib import ExitStack
import concourse.bass as bass
import concourse.mybir as mybir
from concourse.bass_utils import run_bass_kernel_spmd

F32 = mybir.dt.float32
BF16 = mybir.dt.bfloat16
AF = mybir.ActivationFunctionType
ALU = mybir.AluOpType

ENGS = ("pe", "act", "dve", "pool", "sp")

D = 1024
NB_SEQ = 4
SEQ = 2048
DFF = 2816
NFF = 22
EPS = 1e-6
QSCALE = 128 ** -0.5


def I(name, *args, **kwargs):
    return (name, args, kwargs)


class Res:
    __slots__ = ("name", "last_w", "readers")

    def __init__(self, name):
        self.name = name
        self.last_w = None
        self.readers = []


class Op:
    __slots__ = ("eng", "fn", "deps", "needs_inc", "ms", "dma", "dsem", "dval")


class Prog:
    def __init__(self, nc, same_engine_sync=True):
        self.nc = nc
        self.ops = {e: [] for e in ENGS}
        self.same_engine_sync = same_engine_sync
        self.dma_sems = {}

    def add(self, eng, fn, reads=(), writes=(), dma_key=None):
        op = Op()
        op.eng = eng
        op.fn = fn
        op.needs_inc = False
        op.ms = None
        op.dma = dma_key is not None
        op.dsem = None
        op.dval = None
        deps = []
        seen = set()

        def _add(d):
            if d is None or id(d) in seen:
                return
            seen.add(id(d))
            deps.append(d)

        for r in reads:
            _add(r.last_w)
        for w in writes:
            _add(w.last_w)
            for rd in w.readers:
                _add(rd)
        for r in reads:
            r.readers.append(op)
        for w in writes:
            w.last_w = op
            w.readers = []
        fdeps = []
        for d in deps:
            if d is op:
                continue
            if not d.dma and d.eng == eng:
                if eng == "pe" or not self.same_engine_sync:
                    continue
            fdeps.append(d)
        op.deps = fdeps
        for d in fdeps:
            d.needs_inc = True
        if op.dma:
            ent = self.dma_sems.setdefault(dma_key, [None, 0])
            ent[1] += 16
            op.dsem = dma_key
            op.dval = ent[1]
        self.ops[eng].append(op)
        return op

    def emit(self, stack, final_wait_eng="sp"):
        nc = self.nc
        esem = {e: stack.enter_context(nc.semaphore("s_" + e)) for e in ENGS}
        for k, ent in self.dma_sems.items():
            ent[0] = stack.enter_context(nc.semaphore("d_%s" % (k,)))
        for e in ENGS:
            c = 0
            for op in self.ops[e]:
                if op.dma:
                    continue
                if op.needs_inc:
                    c += 1
                    op.ms = c
        block = stack.enter_context(nc.Block())

        def make(e):
            def body(eng):
                waited = {}
                for op in self.ops[e]:
                    need = {}
                    for d in op.deps:
                        if d.dma:
                            sem, val = self.dma_sems[d.dsem][0], d.dval
                            key = ("d", d.dsem)
                        else:
                            sem, val = esem[d.eng], d.ms
                            key = ("e", d.eng)
                        if key not in need or need[key][1] < val:
                            need[key] = (sem, val)
                    for key, (sem, val) in need.items():
                        if waited.get(key, 0) < val:
                            eng.wait_ge(sem, val)
                            waited[key] = val
                    ins = op.fn(eng) if callable(op.fn) else getattr(eng, op.fn[0])(*op.fn[1], **op.fn[2])
                    if op.dma:
                        ins.then_inc(self.dma_sems[op.dsem][0], 16)
                    elif op.needs_inc:
                        ins.then_inc(esem[e], 1)
                if e == final_wait_eng:
                    for k, ent in self.dma_sems.items():
                        if waited.get(("d", k), 0) < ent[1]:
                            eng.wait_ge(ent[0], ent[1])
                    for e2 in ENGS:
                        c = max([o.ms or 0 for o in self.ops[e2]] + [0])
                        if c and e2 != e and waited.get(("e", e2), 0) < c:
                            eng.wait_ge(esem[e2], c)
            return body

        block.tensor(make("pe"))
        block.scalar(make("act"))
        block.vector(make("dve"))
        block.gpsimd(make("pool"))
        block.sync(make("sp"))


def _consts():
    s = np.arange(128)[:, None]
    t = np.arange(128)[None, :]
    same = (s // 64) == (t // 64)
    maskN = (s <= t).astype(np.float32)
    maskS = ((s <= t) & same).astype(np.float32)
    triN = maskN * (-1.0 / 16.0)
    triS = maskS * (-1.0 / 16.0)
    cf = np.concatenate([maskN, maskS, triN, triS], axis=1).astype(np.float32)
    mats = [np.eye(128, dtype=np.float32)]
    for w in (2, 4, 8, 16):
        mats.append(((s <= t) & (s > t - w)).astype(np.float32) / w - np.eye(128, dtype=np.float32))
    for w in (2, 4, 8, 16):
        mats.append(((s - 128) > (t - w)).astype(np.float32) / w)
    for w in (2, 4, 8, 16):
        mats.append(((s <= t) & (s > t - w) & same).astype(np.float32) / w - np.eye(128, dtype=np.float32))
    for w in (2, 4, 8, 16):
        m = np.zeros((128, 128), np.float32)
        m += (((s - 128) > (t - w)) & (t < 64) & (s >= 113)).astype(np.float32) / w
        m += (((s - 64) > (t - 64 - w)) & (t >= 64) & (s >= 49) & (s < 64)).astype(np.float32) / w
        mats.append(m)
    cb = np.concatenate(mats, axis=1).astype(np.float32)
    return cf, cb


C_Q, C_K, C_V, C_R, C_A, C_P, C_GA, C_GB = 0, 512, 1024, 2048, 3072, 3088, 3600, 4624
IN_SLABS = [C_Q, C_K, C_V, C_V + 512, C_P, C_R, C_R + 512, C_GA, C_GA + 512, C_GB, C_GB + 512]
N_SLABS = 28


def build_program(n_tiles_limit=None):
    nc = bass.Bass("TRN2", target_bir_lowering=False)

    def din(name, shape, dt=F32):
        return nc.dram_tensor(name, list(shape), dt, kind="ExternalInput").ap()

    def dout(name, shape):
        return nc.dram_tensor(name, list(shape), F32, kind="ExternalOutput").ap()

    xp = din("xp", [NB_SEQ, SEQ, D])
    xs = din("xs", [128, D])
    sg_in = din("sg_in", [2, 4, 128, 256])
    sp_in = din("sp_in", [2, 15, 512])
    meta = din("meta", [16, D])
    g1c = din("g1c", [128, 8])
    g2c = din("g2c", [128, 8])
    w_in = din("w_in", [D, 5648])
    w_a2 = din("w_a2", [16, 512])
    b_al = din("b_al", [1, 512])
    gla_g = din("gla_g", [D])
    w_pool = din("w_pool", [4, 128, 256])
    pscale = din("pscale", [D])
    w_out = din("w_out", [D, D])
    w_fg = din("w_fg", [D, DFF])
    w_fu = din("w_fu", [D, DFF])
    w_fd = din("w_fd", [DFF, D])
    gf = din("gf", [D])
    cst_f = din("cst_f", [128, 512])
    cst_b = din("cst_b", [128, 17 * 128])

    yp = dout("yp", [NB_SEQ, SEQ, D])
    ys = dout("ys", [128, D])
    sgp = dout("sgp", [NB_SEQ, 4, 128, 256])
    spp = dout("spp", [NB_SEQ, 15, 512])
    sgs = dout("sgs", [2, 4, 128, 256])
    sps = dout("sps", [2, 15, 512])

    wsl = nc.dram_tensor("wsl", [N_SLABS, 128, 8, 512], BF16).ap()

    P = Prog(nc)
    with ExitStack() as st:
        def sb(name, shape, dt):
            return st.enter_context(nc.sbuf_tensor(name, list(shape), dt))

        def ps(name, shape, dt):
            return st.enter_context(nc.psum_tensor(name, list(shape), dt))

        slab = [sb("slab%d" % i, [128, 8, 512], BF16) for i in range(3)]
        wout_sb = sb("wout_sb", [128, 8, 1024], BF16)
        wpool_sb = sb("wpool_sb", [128, 4, 256], BF16)
        wa_sb = sb("wa_sb", [128, 8, 16], BF16)
        w2aug = sb("w2aug", [33, 512], BF16)
        xh2 = [sb("xhA", [128, 4, 1024], F32), sb("xhB", [128, 4, 1024], F32)]
        xb2 = [sb("xb0", [128, 1024], BF16), sb("xb1", [128, 1024], BF16)]
        ybuf_t = sb("ybuf", [128, 512], F32)
        junk = sb("junk", [128, 1024], BF16)
        ssA = sb("ssA", [128, 4], F32)
        rstdA = sb("rstdA", [128, 4], F32)
        xnT = sb("xnT", [128, 8, 512], BF16)
        qTt = sb("qTt", [128, 4, 512], BF16)
        kTt = sb("kTt", [128, 4, 512], BF16)
        regX = sb("regX", [128, 12288], BF16)
        regY = sb("regY", [128, 4096], F32)
        gt = sb("gt", [128, 512], F32)
        a_aug = sb("a_aug", [33, 512], BF16)
        p_sb = sb("p_sb", [128, 5, 512], BF16)
        p_meta = sb("p_meta", [128, 512], BF16)
        p_hist = sb("p_hist", [128, 512], BF16)
        S_f = [sb("S_f%d" % i, [128, 4, 256], F32) for i in range(3)]
        S_b = [sb("S_b%d" % i, [128, 4, 256], BF16) for i in range(3)]
        ident = sb("ident", [128, 128], BF16)
        attn4 = sb("attn4", [128, 4, 512], BF16)
        ktok4 = sb("ktok4", [128, 4, 512], BF16)
        mixT4 = sb("mixT4", [128, 4, 512], BF16)
        S_ring = [sb("S_ring%d" % i, [128, 4, 256], BF16) for i in range(5)]
        Mmat = sb("Mmat", [128, 16, 128], BF16)
        cf_sb = sb("cf_sb", [128, 4, 128], F32)
        gam_rep = sb("gam_rep", [128, 1024], F32)
        gf_rep = sb("gf_rep", [128, 1024], F32)
        g1_sb = sb("g1_sb", [128, 8], F32)
        g2_sb = sb("g2_sb", [128, 8], F32)
        ss = sb("ss", [128, 4], F32)
        rstd = sb("rstd", [128, 4], F32)
        ssq = sb("ssq", [128, 4], F32)
        rsq = sb("rsq", [128, 4], F32)
        elast = sb("elast", [128, 4, 2, 4], F32)
        neghalf = sb("neghalf", [128, 4], F32)

        v_sb = regX[:, 0:4096].rearrange("p (j c) -> p j c", j=4)
        Ga = regX[:, 4096:8192].rearrange("p (j c) -> p j c", j=4)
        Gb = regX[:, 8192:12288].rearrange("p (j c) -> p j c", j=4)
        uT = regX[:, 0:NFF * 512].rearrange("p (f n) -> p f n", f=NFF)
        Eb = regY[:, 0:2048].rearrange("p (h n) -> p h n", h=4)
        Enb = regY[:, 2048:4096].rearrange("p (h n) -> p h n", h=4)
        bufA = regY[:, 0:1024]
        bufB = regY[:, 1024:2048]
        merged = regY[:, 2048:2560].bitcast(BF16)
        mT_sb = regY[:, 2560:3072].bitcast(BF16).rearrange("p (k t) -> p k t", k=8)
        attn_sb = regY[:, 3072:3328].bitcast(BF16).rearrange("p (h t) -> p h t", h=4)
        ktok_sb = regY[:, 3328:3584].bitcast(BF16).rearrange("p (h t) -> p h t", h=4)
        mixT_sb = regY[:, 3584:3840].bitcast(BF16).rearrange("p (h t) -> p h t", h=4)
        merged2 = [merged, regY[:, 3072:3584].bitcast(BF16)]
        sg_t = regY[:, 0:512]
        p_f32 = ybuf_t[:]
        wp_stage = regY[:, 2048:3072].rearrange("p (g c) -> p g c", g=4)

        bank0 = ps("bank0", [128, 512], F32)
        bank1 = ps("bank1", [128, 512], F32)
        pr23 = ps("pr23", [128, 1024], F32)
        pr45 = ps("pr45", [128, 1024], F32)
        pr67 = ps("pr67", [128, 1024], F32)
        bank0_bf = bank0[:].bitcast(BF16)
        bank1_bf = bank1[:].bitcast(BF16)

        RB = [Res("bank%d" % i) for i in range(8)]
        R_slab = [Res("slab%d" % i) for i in range(3)]
        R_wsl = [Res("wsl%d" % i) for i in range(N_SLABS)]
        R_wslb = [Res("wslb%d" % i) for i in range(N_SLABS)]
        R_X = [Res("X%d" % i) for i in range(12)]
        R_Y = [Res("Y%d" % i) for i in range(16)]
        R = {}

        def r(name):
            if name not in R:
                R[name] = Res(name)
            return R[name]

        R_xh2 = [[Res("xh%d_%d" % (i, j)) for j in range(4)] for i in range(2)]
        R_xb = [Res("xb0"), Res("xb1")]
        R_ssA = [Res("ssA%d" % j) for j in range(4)]
        R_ss = [Res("ss%d" % j) for j in range(4)]
        R_junk = [Res("junk%d" % j) for j in range(4)]
        R_xnT = [Res("xnT%d" % j) for j in range(4)]
        R_q = [[Res("q%d_%d" % (j, h)) for h in range(4)] for j in range(4)]
        R_k = [[Res("k%d_%d" % (j, h)) for h in range(4)] for j in range(4)]
        R_p = [Res("p%d" % j) for j in range(5)]
        R_S = [[Res("S%d_%d" % (i, h)) for h in range(4)] for i in range(3)]
        R_Sb = [Res("Sb%d" % i) for i in range(3)]
        R_ring = [[Res("ring%d_%d" % (i, h)) for h in range(4)] for i in range(5)]
        R_ssq = [Res("ssq%d" % h) for h in range(4)]
        R_attn = [Res("attn%d" % i) for i in range(4)]
        R_ktok = [Res("ktok%d" % i) for i in range(4)]
        R_mixT = [Res("mixT%d" % i) for i in range(4)]
        ring = [0]
        RY_Eb = R_Y[0:8]
        RY_Enb = R_Y[8:16]
        RY_bufA = R_Y[0:4]
        RY_bufB = R_Y[4:8]
        RY_merged = R_Y[8:10]
        RY_mT = R_Y[10:12]
        RY_attn = R_Y[12:13]
        RY_ktok = R_Y[13:14]
        RY_mixT = R_Y[14:15]
        RY_merged2 = [R_Y[8:10], R_Y[12:14]]
        RY_sg = R_Y[0:2]
        RY_ybuf = [r("ybuf")]

        def PE(fn, reads=(), writes=()):
            return P.add("pe", fn, reads, writes)

        def ACT(fn, reads=(), writes=()):
            return P.add("act", fn, reads, writes)

        def DVE(fn, reads=(), writes=()):
            return P.add("dve", fn, reads, writes)

        def POOL(fn, reads=(), writes=()):
            return P.add("pool", fn, reads, writes)

        dma_ctr = [0]

        def DMA(q, fn, reads=(), writes=(), key=None):
            if key is None:
                dma_ctr[0] += 1
                key = "u%d" % dma_ctr[0]
            return P.add(q, fn, reads, writes, dma_key=key)

        tiles = [[dict(kind="meta"), dict(kind="sample")]]
        for s in range(NB_SEQ):
            for tt in range(4):
                tiles.append([dict(kind="prompt", seq=s, blk=tt * 4 + j) for j in range(4)])
        if n_tiles_limit is not None:
            tiles = tiles[:n_tiles_limit]


        def load_x(ti):
            xh_t, R_xh_t = xh2[ti % 2], R_xh2[ti % 2]
            for j, b in enumerate(tiles[ti]):
                kx = "xh%d_%d" % (ti % 2, j)
                if b["kind"] == "meta":
                    POOL(I("memset", xh_t[:, j, :], 0.0), writes=[R_xh_t[j]])
                    DMA("pool", I("dma_start", out=xh_t[112:128, j, :], in_=meta), writes=[R_xh_t[j]], key=kx)
                elif b["kind"] == "sample":
                    DMA("pool", I("dma_start", out=xh_t[:, j, :], in_=xs), writes=[R_xh_t[j]], key=kx)
                else:
                    t0 = b["blk"] * 128
                    DMA("pool", I("dma_start", out=xh_t[:, j, :], in_=xp[b["seq"], t0:t0 + 128, :]),
                        writes=[R_xh_t[j]], key=kx)

        DMA("pool", I("dma_start", out=ident[:], in_=cst_b[:, 0:128]), writes=[r("ident")])
        DMA("pool", I("dma_start", out=Mmat[:], in_=cst_b[:, 128:17 * 128].rearrange("p (m t) -> p m t", m=16)), writes=[r("Mmat")])
        DMA("sp", I("dma_start", out=cf_sb[:], in_=cst_f.rearrange("p (m t) -> p m t", m=4)), writes=[r("cf")])
        DMA("sp", I("dma_start", out=g1_sb[:], in_=g1c), writes=[r("g1")])
        DMA("sp", I("dma_start", out=g2_sb[:], in_=g2c), writes=[r("g2")])
        DMA("sp", I("dma_start", out=gam_rep[:], in_=gla_g.partition_broadcast(128)), writes=[r("gam")])
        DMA("sp", I("dma_start", out=gf_rep[:], in_=gf.partition_broadcast(128)), writes=[r("gfr")])
        DMA("sp", I("dma_start", out=bufA, in_=pscale.partition_broadcast(128)), writes=RY_bufA)
        DMA("sp", I("dma_start", out=wp_stage, in_=w_pool.rearrange("g c d -> c g d")), writes=R_Y[8:12])
        DVE(I("tensor_tensor", out=wpool_sb[:], in0=wp_stage, in1=bufA.rearrange("p (g c) -> p g c", g=4), op=ALU.mult),
            reads=RY_bufA + R_Y[8:12], writes=[r("wpool")])
        DMA("pool", I("dma_start", out=wa_sb[:], in_=w_in[:, C_A:C_A + 16].rearrange("(k p) c -> p k c", p=128)), writes=[r("wa")])
        POOL(I("memset", w2aug[:], 0.0), writes=[r("w2aug")])
        POOL(I("memset", neghalf[:], -0.5), writes=[r("neghalf")])
        DMA("pool", I("dma_start", out=w2aug[0:16, :], in_=w_a2), writes=[r("w2aug")])
        DMA("pool", I("dma_start", out=w2aug[32:33, :], in_=b_al), writes=[r("w2aug")])
        POOL(I("memset", a_aug[:], 0.0), writes=[r("a_aug")])
        POOL(I("memset", a_aug[32:33, :], 1.0), writes=[r("a_aug")])
        POOL(I("memset", p_hist[:], 0.0), writes=[r("p_hist")])
        DMA("pool", I("dma_start", out=p_hist[113:128, :], in_=sp_in[0]), writes=[r("p_hist")])
        DMA("pool", I("dma_start", out=p_hist[49:64, :], in_=sp_in[1]), writes=[r("p_hist")])

        def cast_slab(i):
            if i < 11:
                c0 = IN_SLABS[i]
                DMA("pool", I("dma_start", out=wsl[i], in_=w_in[:, c0:c0 + 512].rearrange("(k p) c -> p k c", p=128)),
                    writes=[R_wsl[i]])
            elif i < 22:
                jj = i - 11
                DMA("pool", I("dma_start", out=wsl[i][:, :, 0:256], in_=w_fg[:, jj * 256:(jj + 1) * 256].rearrange("(k p) c -> p k c", p=128)),
                    writes=[R_wsl[i]])
                DMA("pool", I("dma_start", out=wsl[i][:, :, 256:512], in_=w_fu[:, jj * 256:(jj + 1) * 256].rearrange("(k p) c -> p k c", p=128)),
                    writes=[R_wslb[i]])
            else:
                half, jj = divmod(i - 22, 3)
                n = 8 if jj < 2 else 6
                src = w_fd.rearrange("(f p) c -> p f c", p=128)[:, jj * 8:jj * 8 + n, half * 512:(half + 1) * 512]
                DMA("pool", I("dma_start", out=wsl[i][:, 0:n, :], in_=src), writes=[R_wsl[i]])

        POOL(I("memset", S_f[1][:], 0.0), writes=R_S[1])
        POOL(I("memset", S_ring[0][:], 0.0), writes=R_ring[0])
        DMA("sp", I("dma_start", out=S_f[0][:], in_=sg_in[0].rearrange("h k v -> k h v")), writes=R_S[0], key="S0")
        DMA("sp", I("dma_start", out=S_f[2][:], in_=sg_in[1].rearrange("h k v -> k h v")), writes=R_S[2], key="S2")
        ACT(I("activation", out=S_b[0][:], in_=S_f[0][:], func=AF.Copy), reads=R_S[0], writes=[R_Sb[0]])
        ACT(I("activation", out=S_b[2][:], in_=S_f[2][:], func=AF.Copy), reads=R_S[2], writes=[R_Sb[2]])
        load_x(0)
        for i in range(11):
            cast_slab(i)
        for kc in range(8):
            DMA("pool", I("dma_start", out=wout_sb[:, kc, :], in_=w_out[kc * 128:(kc + 1) * 128, :]), writes=[r("wout%d" % kc)])
        cast_done = [11]

        def cast_upto(k):
            while cast_done[0] <= k and cast_done[0] < N_SLABS:
                cast_slab(cast_done[0])
                cast_done[0] += 1


        slab_seq = [0]
        loaded = [0]
        total_slabs = len(tiles) * N_SLABS

        def ensure(idx):
            while loaded[0] <= idx and loaded[0] < total_slabs:
                i = loaded[0]
                sid = i % N_SLABS
                b = i % 3
                nv = 6 if sid in (24, 27) else 8
                DMA("sp", I("dma_start", out=slab[b][:, 0:nv, :], in_=wsl[sid][:, 0:nv, :]),
                    reads=[R_wsl[sid], R_wslb[sid]], writes=[R_slab[b]], key="slab%d" % b)
                loaded[0] += 1

        def next_slab(prefetch=2):
            idx = slab_seq[0]
            if idx < N_SLABS:
                cast_upto(idx + 4)
            ensure(idx + prefetch)
            slab_seq[0] += 1
            b = idx % 3
            return slab[b], R_slab[b]

        psum_tok = [(bank0[:], [RB[0]]), (bank1[:], [RB[1]]), (pr23[:, 0:512], [RB[2]]), (pr23[:, 512:1024], [RB[3]])]
        tok_ctr = [0]

        def next_tok_psum():
            pr = psum_tok[tok_ctr[0] % 4]
            tok_ctr[0] += 1
            return pr

        def rms_rstd(src_ss, dst, c0, c1, scale, rs, ws):
            ACT(I("activation", out=dst[:, c0:c1], in_=src_ss[:, c0:c1], func=AF.Ln, scale=scale, bias=EPS), reads=rs, writes=ws)
            ACT(I("activation", out=dst[:, c0:c1], in_=dst[:, c0:c1], func=AF.Exp, scale=-0.5), reads=ws, writes=ws)

        def rms_rstd_pow(src_ss, dst, c0, c1, scale, rs, ws):
            DVE(I("tensor_scalar", out=dst[:, c0:c1], in0=src_ss[:, c0:c1], scalar1=scale, scalar2=EPS, op0=ALU.mult, op1=ALU.add), reads=rs, writes=ws)
            POOL(I("tensor_tensor", out=dst[:, c0:c1], in0=dst[:, c0:c1], in1=neghalf[:, 0:c1 - c0], op=ALU.pow), reads=ws + [r("neghalf")], writes=ws)

        def nt_act(xh_t, R_xh_t, j, rstd_t, rres, par):
            ACT(I("activation", out=xb2[par][:], in_=xh_t[:, j, :], func=AF.Copy, scale=rstd_t[:, j:j + 1]),
                reads=[R_xh_t[j]] + rres, writes=[R_xb[par]])

        def nt_pe(j, gsb, gres, par):
            pT = bank0_bf.rearrange("p (k t) -> p k t", k=8)
            for k in range(8):
                PE(I("transpose", out=pT[:, k, :], in_=xb2[par][:, k * 128:(k + 1) * 128], identity=ident[:]),
                   reads=[R_xb[par], r("ident")], writes=[RB[0]])
            DVE(I("tensor_tensor", out=xnT[:, :, j * 128:(j + 1) * 128], in0=pT,
                  in1=gsb[:].unsqueeze(2).to_broadcast([128, 8, 128]), op=ALU.mult),
                reads=[RB[0], gres], writes=[R_xnT[j]])

        def phase_AB12(ti):
            blocks = tiles[ti]
            nb = len(blocks)
            N = nb * 128
            xh, R_xh = xh2[ti % 2], R_xh2[ti % 2]
            for j in range(nb):
                ACT(I("activation", out=junk[:], in_=xh[:, j, :], func=AF.Square, accum_out=ssA[:, j:j + 1]),
                    reads=[R_xh[j]], writes=[R_ssA[j]] + R_junk)
                yield "early"
            rms_rstd(ssA, rstdA, 0, nb, 1.0 / D, R_ssA[0:nb], [r("rstdA")])
            yield "early"
            nt_act(xh, R_xh, 0, rstdA, [r("rstdA")], 0)
            yield "early_last"
            for j in range(nb):
                if j + 1 < nb:
                    nt_act(xh, R_xh, j + 1, rstdA, [r("rstdA")], (j + 1) % 2)
                nt_pe(j, g1_sb, r("g1"), j % 2)
                yield
            pA = pr23[0:16, 0:N]
            for kc in range(8):
                PE(I("matmul", pA, lhsT=wa_sb[:, kc, :], rhs=xnT[:, kc, 0:N], start=(kc == 0), stop=(kc == 7)),
                   reads=[r("wa")] + R_xnT[0:nb], writes=[RB[2]])
            ACT(I("activation", out=a_aug[0:16, 0:N], in_=pA, func=AF.Copy), reads=[RB[2]], writes=[r("a_aug")])
            yield
            for j, b in enumerate(blocks):
                smp = b["kind"] == "sample"
                tri = cf_sb[:, 3, :] if smp else cf_sb[:, 2, :]
                pZ = pr23[:, 512:1024]
                PE(I("matmul", pZ, lhsT=a_aug[0:33, j * 128:(j + 1) * 128], rhs=w2aug[0:33, :], start=True, stop=True),
                   reads=[r("a_aug"), r("w2aug")], writes=[RB[3]])
                ACT(I("activation", out=gt[:], in_=pZ, func=AF.Exp, scale=-1.0), reads=[RB[3]], writes=[r("gt")])
                ACT(I("activation", out=gt[:], in_=gt[:], func=AF.Ln, bias=1.0), reads=[r("gt")], writes=[r("gt")])
                yield
                pBk = bank1[:].rearrange("p (h t) -> p h t", h=4)
                for h in range(4):
                    PE(I("matmul", pBk[:, h, :], lhsT=gt[:, h * 128:(h + 1) * 128], rhs=tri, start=True, stop=True),
                       reads=[r("gt"), r("cf")], writes=[RB[1]])
                ACT(I("activation", out=Eb[:, :, j * 128:(j + 1) * 128], in_=pBk, func=AF.Exp),
                    reads=[RB[1]], writes=RY_Eb)
                ACT(I("activation", out=Enb[:, :, j * 128:(j + 1) * 128], in_=pBk, func=AF.Exp, scale=-1.0),
                    reads=[RB[1]], writes=RY_Enb)
                DVE(I("tensor_copy", out=elast[:, j, 1, :], in_=Eb[:, :, j * 128 + 127]), reads=RY_Eb, writes=[r("elast")])
                if smp:
                    DVE(I("tensor_copy", out=elast[:, j, 0, :], in_=Eb[:, :, j * 128 + 63]), reads=RY_Eb, writes=[r("elast")])
                yield

        for _ in phase_AB12(0):
            pass

        pending_d4 = []
        for ti, blocks in enumerate(tiles):
            nb = len(blocks)
            N = nb * 128
            xh, R_xh = xh2[ti % 2], R_xh2[ti % 2]
            first_of_seq = blocks[0]["kind"] == "prompt" and blocks[0]["blk"] == 0
            if ti == 1 or first_of_seq:
                if ti >= 1:
                    POOL(I("tensor_copy", out=S_f[0][:], in_=S_f[1][:]), reads=R_S[1], writes=R_S[0])
                    POOL(I("tensor_copy", out=S_ring[ring[0] % 5][:], in_=S_b[1][:]), reads=[R_Sb[1]], writes=R_ring[ring[0] % 5])
                    POOL(I("tensor_copy", out=p_sb[:, 0, :], in_=p_meta[:]), reads=[r("p_meta")], writes=[R_p[0]])

            for which in range(2):
                sl, rsl = next_slab()
                dst = qTt if which == 0 else kTt
                E = Eb if which == 0 else Enb
                RE = RY_Eb if which == 0 else RY_Enb
                sc = QSCALE if which == 0 else 1.0
                for h in range(4):
                    pq, rpq = next_tok_psum()
                    for kc in range(8):
                        PE(I("matmul", pq[:, 0:N], lhsT=sl[:, kc, h * 128:(h + 1) * 128], rhs=xnT[:, kc, 0:N],
                                                                        start=(kc == 0), stop=(kc == 7)),
                           reads=[rsl] + R_xnT[0:nb], writes=rpq)
                    DVE(I("scalar_tensor_tensor", out=dst[:, h, 0:N], in0=pq[:, 0:N], scalar=sc, in1=E[:, h, 0:N],
                                                                                           op0=ALU.mult, op1=ALU.mult),
                        reads=rpq + RE, writes=[(R_q if which == 0 else R_k)[jj][h] for jj in range(nb)])

            for fn in pending_d4:
                fn()
            del pending_d4[:]

            for c in range(2):
                sl, rsl = next_slab()
                for j in range(nb):
                    pv, rpv = next_tok_psum()
                    for kc in range(8):
                        PE(I("matmul", pv, lhsT=xnT[:, kc, j * 128:(j + 1) * 128], rhs=sl[:, kc, :],
                                                                        start=(kc == 0), stop=(kc == 7)),
                           reads=[rsl, R_xnT[j]], writes=rpv)
                    ACT(I("activation", out=v_sb[:, j, c * 512:(c + 1) * 512], in_=pv, func=AF.Copy),
                        reads=rpv, writes=[R_X[j]])
            sl, rsl = next_slab()
            for j, b in enumerate(blocks):
                pv, rpv = next_tok_psum()
                for kc in range(8):
                    PE(I("matmul", pv, lhsT=xnT[:, kc, j * 128:(j + 1) * 128], rhs=sl[:, kc, :],
                                                                    start=(kc == 0), stop=(kc == 7)),
                       reads=[rsl, R_xnT[j]], writes=rpv)
                DVE(I("tensor_copy", out=p_sb[:, j + 1, :], in_=pv), reads=rpv, writes=[R_p[j + 1]])
                last_blk = (b["kind"] == "sample") or (b["kind"] == "prompt" and b["blk"] == 15)
                if b["kind"] == "meta":
                    POOL(I("tensor_copy", out=p_meta[:], in_=p_sb[:, j + 1, :]), reads=[R_p[j + 1]], writes=[r("p_meta")])
                if last_blk:
                    DVE(I("tensor_copy", out=p_f32, in_=pv), reads=rpv, writes=[r("ybuf")])
                    if b["kind"] == "sample":
                        DMA("pool", I("dma_start", out=sps[0], in_=p_f32[49:64, :]), reads=[r("ybuf")], key="pf")
                        DMA("pool", I("dma_start", out=sps[1], in_=p_f32[113:128, :]), reads=[r("ybuf")], key="pf")
                    else:
                        DMA("pool", I("dma_start", out=spp[b["seq"]], in_=p_f32[113:128, :]), reads=[r("ybuf")], key="pf")

            if ti + 1 < len(tiles):
                load_x(ti + 1)

            blk_info = []
            for j, b in enumerate(blocks):
                smp = b["kind"] == "sample"
                js = slice(j * 128, (j + 1) * 128)
                mask = cf_sb[:, 1, :] if smp else cf_sb[:, 0, :]
                if b["kind"] == "meta" or smp:
                    pprev, rpprev = p_hist[:], r("p_hist")
                else:
                    pprev, rpprev = p_sb[:, j, :], R_p[j]
                mc0 = 8 if smp else 0
                mp0 = 12 if smp else 4
                par = j % 2
                pAt = (bank0 if par == 0 else bank1)[:].rearrange("p (h t) -> p h t", h=4)
                rAt = [RB[par]]
                a4 = attn4[:, j, :].rearrange("p (h t) -> p h t", h=4)
                for h in range(4):
                    PE(I("matmul", pAt[:, h, :], lhsT=kTt[:, h, js], rhs=qTt[:, h, js], start=True, stop=True),
                       reads=[R_q[j][h], R_k[j][h]], writes=rAt)
                DVE(I("tensor_tensor", out=a4, in0=pAt, in1=mask.unsqueeze(1).to_broadcast([128, 4, 128]), op=ALU.mult),
                    reads=rAt + [r("cf")], writes=[R_attn[j]])
                pKt = pr23[:, par * 512:(par + 1) * 512].bitcast(BF16)[:, 0:512].rearrange("p (h t) -> p h t", h=4)
                rKt = [RB[2 + par]]
                k4 = ktok4[:, j, :].rearrange("p (h t) -> p h t", h=4)
                for h in range(4):
                    PE(I("transpose", out=pKt[:, h, :], in_=kTt[:, h, js], identity=ident[:]),
                       reads=[R_k[j][h], r("ident")], writes=rKt)
                ACT(I("activation", out=k4, in_=pKt, func=AF.Copy), reads=rKt, writes=[R_ktok[j]])
                pMix = pr45[:, par * 512:(par + 1) * 512].rearrange("p (g t) -> p g t", g=4)
                rMix = [RB[4 + par]]
                m4 = mixT4[:, j, :].rearrange("p (g t) -> p g t", g=4)
                for g in range(4):
                    gs = slice(g * 128, (g + 1) * 128)
                    PE(I("matmul", pMix[:, g, :], lhsT=pprev[:, gs], rhs=Mmat[:, mp0 + g, :], start=True, stop=False),
                       reads=[rpprev, r("Mmat")], writes=rMix)
                    PE(I("matmul", pMix[:, g, :], lhsT=p_sb[:, j + 1, gs], rhs=Mmat[:, mc0 + g, :], start=False, stop=True),
                       reads=[R_p[j + 1], r("Mmat")], writes=rMix)
                ACT(I("activation", out=m4, in_=pMix, func=AF.Copy), reads=rMix, writes=[R_mixT[j]])
                blk_info.append(dict(a4=a4, k4=k4, m4=m4, smp=smp))

            def state_step(j, b):
                smp = b["kind"] == "sample"
                k4 = blk_info[j]["k4"]
                if smp:
                    POOL(I("tensor_copy", out=S_b[1][:], in_=S_ring[ring[0] % 5][:]), reads=R_ring[ring[0] % 5], writes=[R_Sb[1]])
                    sts = [(0, slice(0, 64), 0), (2, slice(64, 128), 1)]
                    blk_info[j]["S_in"] = [(S_b[0], [R_Sb[0]] * 4, slice(0, 64)), (S_b[2], [R_Sb[2]] * 4, slice(64, 128))]
                else:
                    sts = [(1 if b["kind"] == "meta" else 0, slice(0, 128), 1)]
                    blk_info[j]["S_in"] = [(S_ring[ring[0] % 5], R_ring[ring[0] % 5], slice(0, 128))]
                pS = pr67[:].rearrange("p (h v) -> p h v", h=4)
                rbs = [RB[6], RB[7]]
                for (si, psl, wh) in sts:
                    for h in range(4):
                        hs = slice(h * 256, (h + 1) * 256)
                        PE(I("matmul", pS[:, h, :], lhsT=k4[psl, h, :], rhs=v_sb[psl, j, hs], start=True, stop=True),
                           reads=[R_ktok[j], R_X[j]], writes=[rbs[h // 2]])
                    DVE(I("tensor_tensor", out=S_f[si][:], in0=S_f[si][:], in1=pS, op=ALU.add),
                        reads=rbs + R_S[si], writes=R_S[si])
                    if not smp:
                        so = (ring[0] + 1) % 5
                        for h in range(4):
                            ACT(I("activation", out=S_ring[so][:, h, :], in_=S_f[si][:, h, :], func=AF.Copy, scale=elast[:, j, wh, h:h + 1]),
                                reads=[R_S[si][h], r("elast")], writes=[R_ring[so][h]])
                    for h in range(4):
                        DVE(I("tensor_scalar", out=S_f[si][:, h, :], in0=S_f[si][:, h, :], scalar1=elast[:, j, wh, h:h + 1], scalar2=None, op0=ALU.mult),
                            reads=[R_S[si][h], r("elast")], writes=[R_S[si][h]])
                if not smp:
                    ring[0] += 1
                if smp:
                    DMA("pool", I("dma_start", out=sgs[0].rearrange("h k v -> k h v"), in_=S_f[0][:]), reads=R_S[0], key="So0")
                    DMA("pool", I("dma_start", out=sgs[1].rearrange("h k v -> k h v"), in_=S_f[2][:]), reads=R_S[2], key="So2")
                elif b["kind"] == "prompt" and b["blk"] == 15:
                    DMA("pool", I("dma_start", out=sgp[b["seq"]].rearrange("h k v -> k h v"), in_=S_f[0][:]), reads=R_S[0], key="So0")

            slab_i = 0
            hbs = [(bufA[:, 0:512], RY_bufA[0:2]), (bufA[:, 512:1024], RY_bufA[2:4]),
                   (bufB[:, 0:512], RY_bufB[0:2]), (bufB[:, 512:1024], RY_bufB[2:4])]
            hbs16 = [(hb_[:, 0:256].bitcast(BF16), rr_) for (hb_, rr_) in hbs]
            hctr = 0
            for kind in range(2):
                for c in range(2):
                    sl, rsl = next_slab()
                    for j in range(nb):
                        pv, rpv = next_tok_psum()
                        for kc in range(8):
                            PE(I("matmul", pv, lhsT=xnT[:, kc, j * 128:(j + 1) * 128], rhs=sl[:, kc, :],
                                 start=(kc == 0), stop=(kc == 7)),
                               reads=[rsl, R_xnT[j]], writes=rpv)
                        cs = slice(c * 512, (c + 1) * 512)
                        hb, rhb = hbs[hctr % 4]
                        hctr += 1
                        if kind == 0:
                            ACT(I("activation", out=hb, in_=pv, func=AF.Sigmoid), reads=rpv, writes=rhb)
                            DVE(I("tensor_tensor", out=Ga[:, j, cs], in0=pv, in1=hb, op=ALU.mult),
                                reads=rpv + rhb, writes=[R_X[4 + j]])
                            POOL(I("tensor_tensor", out=Ga[:, j, cs], in0=Ga[:, j, cs], in1=gam_rep[:, cs], op=ALU.mult),
                                 reads=[R_X[4 + j], r("gam")], writes=[R_X[4 + j]])
                        else:
                            ACT(I("activation", out=hb, in_=pv, func=AF.Sigmoid), reads=rpv, writes=rhb)
                            DVE(I("tensor_tensor", out=Ga[:, j, cs], in0=Ga[:, j, cs], in1=hb, op=ALU.mult),
                                reads=rhb + [R_X[4 + j]], writes=[R_X[4 + j]])
                    if slab_i < nb and slab_i != 3:
                        state_step(slab_i, blocks[slab_i])
                    slab_i += 1
            gb_slabs = [next_slab(), next_slab(prefetch=1)]

            def gb_part(j, c, bk=0):
                sl, rsl = gb_slabs[c]
                pb = {0: bank0[:], 1: bank1[:], 7: pr67[:, 512:1024]}[bk]
                for kc in range(8):
                    PE(I("matmul", pb, lhsT=xnT[:, kc, j * 128:(j + 1) * 128], rhs=sl[:, kc, :],
                         start=(kc == 0), stop=(kc == 7)),
                       reads=[rsl, R_xnT[j]], writes=[RB[bk]])
                ACT(I("activation", out=Gb[:, j, c * 512:(c + 1) * 512], in_=pb, func=AF.Sigmoid),
                    reads=[RB[bk]], writes=[R_X[8 + j]])

            bufA_bf = regY[:, 0:512].bitcast(BF16)
            bufB_bf = regY[:, 1024:1536].bitcast(BF16)
            RY_Abf = R_Y[0:2]
            RY_Bbf = R_Y[4:6]

            def op_T(j):
                mg = merged2[j % 2]
                rmg = RY_merged2[j % 2]
                pMt = bank1_bf.rearrange("p (k t) -> p k t", k=8)
                for k in range(8):
                    PE(I("transpose", out=pMt[:, k, :], in_=mg[:, k * 128:(k + 1) * 128], identity=ident[:]),
                       reads=rmg + [r("ident")], writes=[RB[1]])
                ACT(I("activation", out=mT_sb, in_=pMt, func=AF.Copy), reads=[RB[1]], writes=RY_mT)

            def op_mm(j, halves=(0, 1)):
                pOut = pr67[:]
                for c in halves:
                    for kc in range(8):
                        PE(I("matmul", pOut[:, c * 512:(c + 1) * 512], lhsT=mT_sb[:, kc, :], rhs=wout_sb[:, kc, c * 512:(c + 1) * 512],
                             start=(kc == 0), stop=(kc == 7)),
                           reads=RY_mT + [r("wout%d" % kc)], writes=[RB[6 + c]])
                if 1 in halves:
                    DVE(I("tensor_tensor", out=xh[:, j, :], in0=xh[:, j, :], in1=pOut, op=ALU.add),
                        reads=[RB[6], RB[7], R_xh[j]], writes=[R_xh[j]])

            def d1_act(j):
                ACT(I("activation", out=junk[:], in_=xh[:, j, :], func=AF.Square, accum_out=ss[:, j:j + 1]),
                    reads=[R_xh[j]], writes=[R_ss[j]] + R_junk)
                rms_rstd_pow(ss, rstd, j, j + 1, 1.0 / D, [R_ss[j]], [R_ss[j]])
                nt_act(xh, R_xh, j, rstd, [R_ss[j]], j % 2)

            gb_part(0, 0)
            gb_part(0, 1, bk=1)
            if nb == 4:
                state_step(3, blocks[3])
            for j, b in enumerate(blocks):
                bi = blk_info[j]
                pO = pr23[:]
                for h in range(4):
                    hs = slice(h * 256, (h + 1) * 256)
                    PE(I("matmul", pO[:, hs], lhsT=bi["a4"][:, h, :], rhs=v_sb[:, j, hs], start=True, stop=False),
                       reads=[R_attn[j], R_X[j]], writes=[RB[2 + h // 2]])
                    for (Sb_t, Sb_r, psl) in bi["S_in"]:
                        PE(I("matmul", pO[psl, hs], lhsT=qTt[:, h, j * 128 + psl.start:j * 128 + psl.stop], rhs=Sb_t[:, h, :], start=False, stop=True),
                           reads=[R_q[j][h], Sb_r[h]], writes=[RB[2 + h // 2]])
                pYb = pr45[:]
                rYb = [RB[4], RB[5]]
                for g in range(4):
                    PE(I("matmul", pYb[:, g * 256:(g + 1) * 256], lhsT=bi["m4"][:, g, :], rhs=wpool_sb[:, g, :], start=True, stop=True),
                       reads=[R_mixT[j], r("wpool")], writes=[rYb[g // 2]])
                DVE(I("tensor_tensor", out=bufB_bf, in0=pYb, in1=Gb[:, j, :], op=ALU.mult),
                    reads=rYb + [R_X[8 + j]], writes=RY_Bbf)
                fill = [q for q in {0: [1, 2], 1: [3]}.get(j, []) if q < nb]
                early_T = (j >= 1 and not fill)
                if early_T:
                    op_T(j - 1)
                for h in range(4):
                    hs = slice(h * 256, (h + 1) * 256)
                    ACT(I("activation", out=junk[:, hs], in_=pO[:, hs], func=AF.Square, accum_out=ssq[:, h:h + 1]),
                        reads=[RB[2 + h // 2]], writes=[R_ssq[h], R_junk[h]])
                rms_rstd_pow(ssq, rsq, 0, 4, 1.0 / 256, R_ssq, [r("rsq")])
                for h in range(4):
                    hs = slice(h * 256, (h + 1) * 256)
                    DVE(I("scalar_tensor_tensor", out=bufA_bf[:, hs], in0=pO[:, hs], scalar=rsq[:, h:h + 1], in1=Ga[:, j, hs],
                          op0=ALU.mult, op1=ALU.mult),
                        reads=[RB[2 + h // 2], r("rsq"), R_X[4 + j]], writes=[RY_Abf[h // 2]])
                DVE(I("tensor_tensor", out=merged2[j % 2], in0=bufA_bf, in1=bufB_bf, op=ALU.add), reads=RY_Abf + RY_Bbf, writes=RY_merged2[j % 2])
                if fill:
                    gb_part(fill[0], 0)
                if j >= 1 and not early_T:
                    op_T(j - 1)
                if fill:
                    gb_part(fill[0], 1, bk=7)
                for q in fill[1:]:
                    gb_part(q, 0)
                    gb_part(q, 1, bk=7)
                if j >= 1:
                    op_mm(j - 1)
                if j >= 2:
                    d1_act(j - 2)
                if j >= 3:
                    nt_pe(j - 3, g2_sb, r("g2"), (j - 3) % 2)
            op_T(nb - 1)
            if nb >= 2:
                d1_act(nb - 2)
            if nb >= 3:
                nt_pe(nb - 3, g2_sb, r("g2"), (nb - 3) % 2)
            op_mm(nb - 1, halves=(0,))
            if nb >= 2:
                nt_pe(nb - 2, g2_sb, r("g2"), (nb - 2) % 2)
            op_mm(nb - 1, halves=(1,))
            d1_act(nb - 1)
            N1 = N - 128
            sl0, rsl0 = next_slab()
            ebanks = [(bank1[:], [RB[1]]), (pr23[:, 0:512], [RB[2]]), (pr23[:, 512:1024], [RB[3]]), (pr45[:, 0:512], [RB[4]])]

            def gu_part(c0, c1, rres):
                for cc in range(2):
                    for which in range(2):
                        pb, rb = ebanks[2 * cc + which]
                        w0 = which * 256 + cc * 128
                        for kc in range(8):
                            PE(I("matmul", pb[:, c0:c1], lhsT=sl0[:, kc, w0:w0 + 128], rhs=xnT[:, kc, c0:c1],
                                 start=(kc == 0), stop=(kc == 7)),
                               reads=[rsl0] + rres, writes=rb)

            if N1 > 0:
                gu_part(0, N1, R_xnT[0:nb - 1])
            nt_pe(nb - 1, g2_sb, r("g2"), (nb - 1) % 2)
            gu_part(N1, N, [R_xnT[nb - 1]])
            for cc in range(2):
                pG, rG = ebanks[2 * cc]
                pU, rU = ebanks[2 * cc + 1]
                ACT(I("activation", out=sg_t[:, 0:N], in_=pG[:, 0:N], func=AF.Sigmoid), reads=rG, writes=RY_sg)
                DVE(I("tensor_tensor", out=sg_t[:, 0:N], in0=pG[:, 0:N], in1=sg_t[:, 0:N], op=ALU.mult), reads=rG + RY_sg, writes=RY_sg)
                DVE(I("tensor_tensor", out=uT[:, cc, 0:N], in0=pU[:, 0:N], in1=sg_t[:, 0:N], op=ALU.mult),
                    reads=rU + RY_sg, writes=[R_X[0]])

            if blocks[-1]["kind"] == "prompt" and blocks[-1]["blk"] != 15:
                POOL(I("tensor_copy", out=p_sb[:, 0, :], in_=p_sb[:, nb, :]), reads=[R_p[nb]], writes=[R_p[0]])

            gu = [(bank0[:], [RB[0]]), (bank1[:], [RB[1]]), (pr23[:, 0:512], [RB[2]]), (pr23[:, 512:1024], [RB[3]])]
            gctr = 0
            hoist = None
            early_done = True
            for jj in range(1, 11):
                sl, rsl = next_slab()
                if jj == 7 and ti + 1 < len(tiles):
                    hoist = phase_AB12(ti + 1)
                    early_done = False
                for cc in range(2):
                    ffc = jj * 2 + cc
                    if hoist is not None and not early_done:
                        if next(hoist, None) == "early_last":
                            early_done = True
                    pG, rG = gu[gctr % 4]
                    pU, rU = gu[(gctr + 1) % 4]
                    gctr += 2
                    for kc in range(8):
                        PE(I("matmul", pG[:, 0:N], lhsT=sl[:, kc, cc * 128:(cc + 1) * 128], rhs=xnT[:, kc, 0:N],
                                                                          start=(kc == 0), stop=(kc == 7)),
                           reads=[rsl] + R_xnT[0:nb], writes=rG)
                    for kc in range(8):
                        PE(I("matmul", pU[:, 0:N], lhsT=sl[:, kc, 256 + cc * 128:256 + (cc + 1) * 128], rhs=xnT[:, kc, 0:N],
                                                                          start=(kc == 0), stop=(kc == 7)),
                           reads=[rsl] + R_xnT[0:nb], writes=rU)
                    ACT(I("activation", out=sg_t[:, 0:N], in_=pG[:, 0:N], func=AF.Sigmoid), reads=rG, writes=RY_sg)
                    DVE(I("tensor_tensor", out=sg_t[:, 0:N], in0=pG[:, 0:N], in1=sg_t[:, 0:N], op=ALU.mult), reads=rG + RY_sg, writes=RY_sg)
                    DVE(I("tensor_tensor", out=uT[:, ffc, 0:N], in0=pU[:, 0:N], in1=sg_t[:, 0:N], op=ALU.mult),
                        reads=rU + RY_sg, writes=[R_X[ffc // 2]])
            while hoist is not None and not early_done:
                if next(hoist, None) == "early_last":
                    early_done = True
            dacc = [(pr45[:, 0:512], [RB[4]]), (pr45[:, 512:1024], [RB[5]]), (pr67[:, 0:512], [RB[6]]), (pr67[:, 512:1024], [RB[7]])]
            for half in range(2):
                for jj in range(3):
                    sl, rsl = next_slab()
                    n = 8 if jj < 2 else 6
                    for i in range(n):
                        ffc = jj * 8 + i
                        if hoist is not None and i % 2 == 1:
                            next(hoist, None)
                        for j in range(nb):
                            PE(I("matmul", dacc[j][0], lhsT=uT[:, ffc, j * 128:(j + 1) * 128], rhs=sl[:, i, :],
                                                                            start=(ffc == 0), stop=(ffc == NFF - 1)),
                               reads=[rsl, R_X[ffc // 2]], writes=dacc[j][1])
                for j in range(nb):
                    hsl = slice(half * 512, (half + 1) * 512)
                    DVE(I("tensor_tensor", out=xh[:, j, hsl], in0=xh[:, j, hsl], in1=dacc[j][0], op=ALU.add),
                        reads=dacc[j][1] + [R_xh[j]], writes=[R_xh[j]])
            if hoist is not None:
                for _ in hoist:
                    pass
            def final_norm(ti=ti, blocks=blocks, xh=xh, R_xh=R_xh):
                for j, b in enumerate(blocks):
                    if b["kind"] == "meta":
                        continue
                    ACT(I("activation", out=junk[:], in_=xh[:, j, :], func=AF.Square, accum_out=ss[:, j:j + 1]),
                        reads=[R_xh[j]], writes=[R_ss[j]] + R_junk)
                    rms_rstd(ss, rstd, j, j + 1, 1.0 / D, [R_ss[j]], [R_ss[j]])
                    DVE(I("scalar_tensor_tensor", out=xh[:, j, :], in0=xh[:, j, :], scalar=rstd[:, j:j + 1], in1=gf_rep[:], op0=ALU.mult, op1=ALU.mult),
                        reads=[R_xh[j], R_ss[j], r("gfr")], writes=[R_xh[j]])
                    if b["kind"] == "sample":
                        DMA("pool", I("dma_start", out=ys, in_=xh[:, j, :]), reads=[R_xh[j]], key="yout%d_%d" % (ti % 2, j))
                    else:
                        t0 = b["blk"] * 128
                        DMA("pool", I("dma_start", out=yp[b["seq"], t0:t0 + 128, :], in_=xh[:, j, :]), reads=[R_xh[j]], key="yout%d_%d" % (ti % 2, j))

            pending_d4.append(final_norm)

        for fn in pending_d4:
            fn()
        del pending_d4[:]

        P.emit(st)
    return nc


_NC_CACHE = {}


def _get_nc(limit=None):
    if limit not in _NC_CACHE:
        _NC_CACHE[limit] = build_program(limit)
    return _NC_CACHE[limit]


def kernel(x_prompt, x_sample, state_gla, state_pool, meta_tokens, norm1_g, w_in, w_alpha2, b_alpha,
           gla_norm_g, w_pool, pool_scale, w_out, norm2_g, w_ffn_gate, w_ffn_up, w_ffn_down, norm_f_g,
           _limit=None):
    f = lambda a: np.ascontiguousarray(np.asarray(a, dtype=np.float32))
    cf, cb = _consts()
    nc = _get_nc(_limit)
    shared = {
        "meta": f(meta_tokens),
        "g1c": f(np.asarray(norm1_g).reshape(8, 128).T),
        "g2c": f(np.asarray(norm2_g).reshape(8, 128).T),
        "w_in": f(w_in), "w_a2": f(w_alpha2), "b_al": f(np.asarray(b_alpha).reshape(1, 512)),
        "gla_g": f(gla_norm_g), "w_pool": f(w_pool), "pscale": f(pool_scale), "w_out": f(w_out),
        "w_fg": f(w_ffn_gate), "w_fu": f(w_ffn_up), "w_fd": f(w_ffn_down), "gf": f(norm_f_g),
        "cst_f": cf, "cst_b": cb,
    }
    x_prompt = np.asarray(x_prompt)
    x_sample = np.asarray(x_sample)
    state_gla = np.asarray(state_gla)
    state_pool = np.asarray(state_pool)
    in_maps = []
    for c in range(8):
        m = dict(shared)
        m["xp"] = f(x_prompt[4 * c:4 * c + 4])
        m["xs"] = f(x_sample[2 * c:2 * c + 2].reshape(128, D))
        m["sg_in"] = f(state_gla[2 * c:2 * c + 2])
        m["sp_in"] = f(state_pool[2 * c:2 * c + 2])
        in_maps.append(m)
    res = run_bass_kernel_spmd(nc, in_maps, core_ids=list(range(8)))
    rs = res.results
    y_prompt = np.concatenate([rs[c]["yp"] for c in range(8)], axis=0).astype(np.float32)
    y_sample = np.concatenate([rs[c]["ys"].reshape(2, 64, D) for c in range(8)], axis=0).astype(np.float32)
    sgp = np.concatenate([rs[c]["sgp"] for c in range(8)], axis=0).astype(np.float32)
    spp = np.concatenate([rs[c]["spp"] for c in range(8)], axis=0).astype(np.float32)
    sgs = np.concatenate([rs[c]["sgs"] for c in range(8)], axis=0).astype(np.float32)
    sps = np.concatenate([rs[c]["sps"] for c in range(8)], axis=0).astype(np.float32)
    return (y_prompt, y_sample, sgp, spp, sgs, sps)
```

```python
import os
import numpy as np
from contextlib import ExitStack
import concourse.bass as bass
import concourse.mybir as mybir
from concourse.bass_utils import run_bass_kernel_spmd

F32 = mybir.dt.float32
BF16 = mybir.dt.bfloat16
AF = mybir.ActivationFunctionType
ALU = mybir.AluOpType

ENGS = ("pe", "act", "dve", "pool", "sp")

D = 1024
NB_SEQ = 4
SEQ = 2048
DFF = 2816
NFF = 22
EPS = 1e-6
QSCALE = 128 ** -0.5


def I(name, *args, **kwargs):
    return (name, args, kwargs)


class Res:
    __slots__ = ("name", "last_w", "readers")

    def __init__(self, name):
        self.name = name
        self.last_w = None
        self.readers = []


class Op:
    __slots__ = ("eng", "fn", "deps", "needs_inc", "ms", "dma", "dsem", "dval")


class Prog:
    def __init__(self, nc, same_engine_sync=True):
        self.nc = nc
        self.ops = {e: [] for e in ENGS}
        self.same_engine_sync = same_engine_sync
        self.dma_sems = {}

    def add(self, eng, fn, reads=(), writes=(), dma_key=None):
        op = Op()
        op.eng = eng
        op.fn = fn
        op.needs_inc = False
        op.ms = None
        op.dma = dma_key is not None
        op.dsem = None
        op.dval = None
        deps = []
        seen = set()

        def _add(d):
            if d is None or id(d) in seen:
                return
            seen.add(id(d))
            deps.append(d)

        for r in reads:
            _add(r.last_w)
        for w in writes:
            _add(w.last_w)
            for rd in w.readers:
                _add(rd)
        for r in reads:
            r.readers.append(op)
        for w in writes:
            w.last_w = op
            w.readers = []
        fdeps = []
        for d in deps:
            if d is op:
                continue
            if not d.dma and d.eng == eng:
                if eng == "pe" or not self.same_engine_sync:
                    continue
            fdeps.append(d)
        op.deps = fdeps
        for d in fdeps:
            d.needs_inc = True
        if op.dma:
            ent = self.dma_sems.setdefault(dma_key, [None, 0])
            ent[1] += 16
            op.dsem = dma_key
            op.dval = ent[1]
        self.ops[eng].append(op)
        return op

    def emit(self, stack, final_wait_eng="sp"):
        nc = self.nc
        esem = {e: stack.enter_context(nc.semaphore("s_" + e)) for e in ENGS}
        for k, ent in self.dma_sems.items():
            ent[0] = stack.enter_context(nc.semaphore("d_%s" % (k,)))
        for e in ENGS:
            c = 0
            for op in self.ops[e]:
                if op.dma:
                    continue
                if op.needs_inc:
                    c += 1
                    op.ms = c
        block = stack.enter_context(nc.Block())

        def make(e):
            def body(eng):
                waited = {}
                for op in self.ops[e]:
                    need = {}
                    for d in op.deps:
                        if d.dma:
                            sem, val = self.dma_sems[d.dsem][0], d.dval
                            key = ("d", d.dsem)
                        else:
                            sem, val = esem[d.eng], d.ms
                            key = ("e", d.eng)
                        if key not in need or need[key][1] < val:
                            need[key] = (sem, val)
                    for key, (sem, val) in need.items():
                        if waited.get(key, 0) < val:
                            eng.wait_ge(sem, val)
                            waited[key] = val
                    ins = op.fn(eng) if callable(op.fn) else getattr(eng, op.fn[0])(*op.fn[1], **op.fn[2])
                    if op.dma:
                        ins.then_inc(self.dma_sems[op.dsem][0], 16)
                    elif op.needs_inc:
                        ins.then_inc(esem[e], 1)
                if e == final_wait_eng:
                    for k, ent in self.dma_sems.items():
                        if waited.get(("d", k), 0) < ent[1]:
                            eng.wait_ge(ent[0], ent[1])
                    for e2 in ENGS:
                        c = max([o.ms or 0 for o in self.ops[e2]] + [0])
                        if c and e2 != e and waited.get(("e", e2), 0) < c:
                            eng.wait_ge(esem[e2], c)
            return body

        block.tensor(make("pe"))
        block.scalar(make("act"))
        block.vector(make("dve"))
        block.gpsimd(make("pool"))
        block.sync(make("sp"))


def _consts():
    s = np.arange(128)[:, None]
    t = np.arange(128)[None, :]
    same = (s // 64) == (t // 64)
    maskN = (s <= t).astype(np.float32)
    maskS = ((s <= t) & same).astype(np.float32)
    triN = maskN * (-1.0 / 16.0)
    triS = maskS * (-1.0 / 16.0)
    cf = np.concatenate([maskN, maskS, triN, triS], axis=1).astype(np.float32)
    mats = [np.eye(128, dtype=np.float32)]
    for w in (2, 4, 8, 16):
        mats.append(((s <= t) & (s > t - w)).astype(np.float32) / w - np.eye(128, dtype=np.float32))
    for w in (2, 4, 8, 16):
        mats.append(((s - 128) > (t - w)).astype(np.float32) / w)
    for w in (2, 4, 8, 16):
        mats.append(((s <= t) & (s > t - w) & same).astype(np.float32) / w - np.eye(128, dtype=np.float32))
    for w in (2, 4, 8, 16):
        m = np.zeros((128, 128), np.float32)
        m += (((s - 128) > (t - w)) & (t < 64) & (s >= 113)).astype(np.float32) / w
        m += (((s - 64) > (t - 64 - w)) & (t >= 64) & (s >= 49) & (s < 64)).astype(np.float32) / w
        mats.append(m)
    cb = np.concatenate(mats, axis=1).astype(np.float32)
    return cf, cb


C_Q, C_K, C_V, C_R, C_A, C_P, C_GA, C_GB = 0, 512, 1024, 2048, 3072, 3088, 3600, 4624
IN_SLABS = [C_Q, C_K, C_V, C_V + 512, C_P, C_R, C_R + 512, C_GA, C_GA + 512, C_GB, C_GB + 512]
N_SLABS = 28


def build_program(n_tiles_limit=None):
    nc = bass.Bass("TRN2", target_bir_lowering=False)

    def din(name, shape, dt=F32):
        return nc.dram_tensor(name, list(shape), dt, kind="ExternalInput").ap()

    def dout(name, shape):
        return nc.dram_tensor(name, list(shape), F32, kind="ExternalOutput").ap()

    xp = din("xp", [NB_SEQ, SEQ, D])
    xs = din("xs", [128, D])
    sg_in = din("sg_in", [2, 4, 128, 256])
    sp_in = din("sp_in", [2, 15, 512])
    meta = din("meta", [16, D])
    g1c = din("g1c", [128, 8])
    g2c = din("g2c", [128, 8])
    w_in = din("w_in", [D, 5648])
    w_a2 = din("w_a2", [16, 512])
    b_al = din("b_al", [1, 512])
    gla_g = din("gla_g", [D])
    w_pool = din("w_pool", [4, 128, 256])
    pscale = din("pscale", [D])
    w_out = din("w_out", [D, D])
    w_fg = din("w_fg", [D, DFF])
    w_fu = din("w_fu", [D, DFF])
    w_fd = din("w_fd", [DFF, D])
    gf = din("gf", [D])
    cst_f = din("cst_f", [128, 512])
    cst_b = din("cst_b", [128, 17 * 128])

    yp = dout("yp", [NB_SEQ, SEQ, D])
    ys = dout("ys", [128, D])
    sgp = dout("sgp", [NB_SEQ, 4, 128, 256])
    spp = dout("spp", [NB_SEQ, 15, 512])
    sgs = dout("sgs", [2, 4, 128, 256])
    sps = dout("sps", [2, 15, 512])

    wsl = nc.dram_tensor("wsl", [N_SLABS, 128, 8, 512], BF16).ap()

    P = Prog(nc)
    with ExitStack() as st:
        def sb(name, shape, dt):
            return st.enter_context(nc.sbuf_tensor(name, list(shape), dt))

        def ps(name, shape, dt):
            return st.enter_context(nc.psum_tensor(name, list(shape), dt))

        slab = [sb("slab%d" % i, [128, 8, 512], BF16) for i in range(3)]
        wout_sb = sb("wout_sb", [128, 8, 1024], BF16)
        wpool_sb = sb("wpool_sb", [128, 4, 256], BF16)
        wa_sb = sb("wa_sb", [128, 8, 16], BF16)
        w2aug = sb("w2aug", [33, 512], BF16)
        xh2 = [sb("xhA", [128, 4, 1024], F32), sb("xhB", [128, 4, 1024], F32)]
        xb2 = [sb("xb0", [128, 1024], BF16), sb("xb1", [128, 1024], BF16)]
        ybuf_t = sb("ybuf", [128, 512], F32)
        junk = sb("junk", [128, 1024], BF16)
        ssA = sb("ssA", [128, 4], F32)
        rstdA = sb("rstdA", [128, 4], F32)
        xnT = sb("xnT", [128, 8, 512], BF16)
        qTt = sb("qTt", [128, 4, 512], BF16)
        kTt = sb("kTt", [128, 4, 512], BF16)
        regX = sb("regX", [128, 12288], BF16)
        regY = sb("regY", [128, 4096], F32)
        gt = sb("gt", [128, 512], F32)
        a_aug = sb("a_aug", [33, 512], BF16)
        p_sb = sb("p_sb", [128, 5, 512], BF16)
        p_meta = sb("p_meta", [128, 512], BF16)
        p_hist = sb("p_hist", [128, 512], BF16)
        S_f = [sb("S_f%d" % i, [128, 4, 256], F32) for i in range(3)]
        S_b = [sb("S_b%d" % i, [128, 4, 256], BF16) for i in range(3)]
        ident = sb("ident", [128, 128], BF16)
        attn4 = sb("attn4", [128, 4, 512], BF16)
        ktok4 = sb("ktok4", [128, 4, 512], BF16)
        mixT4 = sb("mixT4", [128, 4, 512], BF16)
        S_ring = [sb("S_ring%d" % i, [128, 4, 256], BF16) for i in range(5)]
        Mmat = sb("Mmat", [128, 16, 128], BF16)
        cf_sb = sb("cf_sb", [128, 4, 128], F32)
        gam_rep = sb("gam_rep", [128, 1024], F32)
        gf_rep = sb("gf_rep", [128, 1024], F32)
        g1_sb = sb("g1_sb", [128, 8], F32)
        g2_sb = sb("g2_sb", [128, 8], F32)
        ss = sb("ss", [128, 4], F32)
        rstd = sb("rstd", [128, 4], F32)
        ssq = sb("ssq", [128, 4], F32)
        rsq = sb("rsq", [128, 4], F32)
        elast = sb("elast", [128, 4, 2, 4], F32)
        neghalf = sb("neghalf", [128, 4], F32)

        v_sb = regX[:, 0:4096].rearrange("p (j c) -> p j c", j=4)
        Ga = regX[:, 4096:8192].rearrange("p (j c) -> p j c", j=4)
        Gb = regX[:, 8192:12288].rearrange("p (j c) -> p j c", j=4)
        uT = regX[:, 0:NFF * 512].rearrange("p (f n) -> p f n", f=NFF)
        Eb = regY[:, 0:2048].rearrange("p (h n) -> p h n", h=4)
        Enb = regY[:, 2048:4096].rearrange("p (h n) -> p h n", h=4)
        bufA = regY[:, 0:1024]
        bufB = regY[:, 1024:2048]
        merged = regY[:, 2048:2560].bitcast(BF16)
        mT_sb = regY[:, 2560:3072].bitcast(BF16).rearrange("p (k t) -> p k t", k=8)
        attn_sb = regY[:, 3072:3328].bitcast(BF16).rearrange("p (h t) -> p h t", h=4)
        ktok_sb = regY[:, 3328:3584].bitcast(BF16).rearrange("p (h t) -> p h t", h=4)
        mixT_sb = regY[:, 3584:3840].bitcast(BF16).rearrange("p (h t) -> p h t", h=4)
        merged2 = [merged, regY[:, 3072:3584].bitcast(BF16)]
        sg_t = regY[:, 0:512]
        p_f32 = ybuf_t[:]
        wp_stage = regY[:, 2048:3072].rearrange("p (g c) -> p g c", g=4)

        bank0 = ps("bank0", [128, 512], F32)
        bank1 = ps("bank1", [128, 512], F32)
        pr23 = ps("pr23", [128, 1024], F32)
        pr45 = ps("pr45", [128, 1024], F32)
        pr67 = ps("pr67", [128, 1024], F32)
        bank0_bf = bank0[:].bitcast(BF16)
        bank1_bf = bank1[:].bitcast(BF16)

        RB = [Res("bank%d" % i) for i in range(8)]
        R_slab = [Res("slab%d" % i) for i in range(3)]
        R_wsl = [Res("wsl%d" % i) for i in range(N_SLABS)]
        R_wslb = [Res("wslb%d" % i) for i in range(N_SLABS)]
        R_X = [Res("X%d" % i) for i in range(12)]
        R_Y = [Res("Y%d" % i) for i in range(16)]
        R = {}

        def r(name):
            if name not in R:
                R[name] = Res(name)
            return R[name]

        R_xh2 = [[Res("xh%d_%d" % (i, j)) for j in range(4)] for i in range(2)]
        R_xb = [Res("xb0"), Res("xb1")]
        R_ssA = [Res("ssA%d" % j) for j in range(4)]
        R_ss = [Res("ss%d" % j) for j in range(4)]
        R_junk = [Res("junk%d" % j) for j in range(4)]
        R_xnT = [Res("xnT%d" % j) for j in range(4)]
        R_q = [[Res("q%d_%d" % (j, h)) for h in range(4)] for j in range(4)]
        R_k = [[Res("k%d_%d" % (j, h)) for h in range(4)] for j in range(4)]
        R_p = [Res("p%d" % j) for j in range(5)]
        R_S = [[Res("S%d_%d" % (i, h)) for h in range(4)] for i in range(3)]
        R_Sb = [Res("Sb%d" % i) for i in range(3)]
        R_ring = [[Res("ring%d_%d" % (i, h)) for h in range(4)] for i in range(5)]
        R_ssq = [Res("ssq%d" % h) for h in range(4)]
        R_attn = [Res("attn%d" % i) for i in range(4)]
        R_ktok = [Res("ktok%d" % i) for i in range(4)]
        R_mixT = [Res("mixT%d" % i) for i in range(4)]
        ring = [0]
        RY_Eb = R_Y[0:8]
        RY_Enb = R_Y[8:16]
        RY_bufA = R_Y[0:4]
        RY_bufB = R_Y[4:8]
        RY_merged = R_Y[8:10]
        RY_mT = R_Y[10:12]
        RY_attn = R_Y[12:13]
        RY_ktok = R_Y[13:14]
        RY_mixT = R_Y[14:15]
        RY_merged2 = [R_Y[8:10], R_Y[12:14]]
        RY_sg = R_Y[0:2]
        RY_ybuf = [r("ybuf")]

        def PE(fn, reads=(), writes=()):
            return P.add("pe", fn, reads, writes)

        def ACT(fn, reads=(), writes=()):
            return P.add("act", fn, reads, writes)

        def DVE(fn, reads=(), writes=()):
            return P.add("dve", fn, reads, writes)

        def POOL(fn, reads=(), writes=()):
            return P.add("pool", fn, reads, writes)

        dma_ctr = [0]

        def DMA(q, fn, reads=(), writes=(), key=None):
            if key is None:
                dma_ctr[0] += 1
                key = "u%d" % dma_ctr[0]
            return P.add(q, fn, reads, writes, dma_key=key)

        tiles = [[dict(kind="meta"), dict(kind="sample")]]
        for s in range(NB_SEQ):
            for tt in range(4):
                tiles.append([dict(kind="prompt", seq=s, blk=tt * 4 + j) for j in range(4)])
        if n_tiles_limit is not None:
            tiles = tiles[:n_tiles_limit]


        def load_x(ti):
            xh_t, R_xh_t = xh2[ti % 2], R_xh2[ti % 2]
            for j, b in enumerate(tiles[ti]):
                kx = "xh%d_%d" % (ti % 2, j)
                if b["kind"] == "meta":
                    POOL(I("memset", xh_t[:, j, :], 0.0), writes=[R_xh_t[j]])
                    DMA("pool", I("dma_start", out=xh_t[112:128, j, :], in_=meta), writes=[R_xh_t[j]], key=kx)
                elif b["kind"] == "sample":
                    DMA("pool", I("dma_start", out=xh_t[:, j, :], in_=xs), writes=[R_xh_t[j]], key=kx)
                else:
                    t0 = b["blk"] * 128
                    DMA("pool", I("dma_start", out=xh_t[:, j, :], in_=xp[b["seq"], t0:t0 + 128, :]),
                        writes=[R_xh_t[j]], key=kx)

        DMA("pool", I("dma_start", out=ident[:], in_=cst_b[:, 0:128]), writes=[r("ident")])
        DMA("pool", I("dma_start", out=Mmat[:], in_=cst_b[:, 128:17 * 128].rearrange("p (m t) -> p m t", m=16)), writes=[r("Mmat")])
        DMA("sp", I("dma_start", out=cf_sb[:], in_=cst_f.rearrange("p (m t) -> p m t", m=4)), writes=[r("cf")])
        DMA("sp", I("dma_start", out=g1_sb[:], in_=g1c), writes=[r("g1")])
        DMA("sp", I("dma_start", out=g2_sb[:], in_=g2c), writes=[r("g2")])
        DMA("sp", I("dma_start", out=gam_rep[:], in_=gla_g.partition_broadcast(128)), writes=[r("gam")])
        DMA("sp", I("dma_start", out=gf_rep[:], in_=gf.partition_broadcast(128)), writes=[r("gfr")])
        DMA("sp", I("dma_start", out=bufA, in_=pscale.partition_broadcast(128)), writes=RY_bufA)
        DMA("sp", I("dma_start", out=wp_stage, in_=w_pool.rearrange("g c d -> c g d")), writes=R_Y[8:12])
        DVE(I("tensor_tensor", out=wpool_sb[:], in0=wp_stage, in1=bufA.rearrange("p (g c) -> p g c", g=4), op=ALU.mult),
            reads=RY_bufA + R_Y[8:12], writes=[r("wpool")])
        DMA("pool", I("dma_start", out=wa_sb[:], in_=w_in[:, C_A:C_A + 16].rearrange("(k p) c -> p k c", p=128)), writes=[r("wa")])
        POOL(I("memset", w2aug[:], 0.0), writes=[r("w2aug")])
        POOL(I("memset", neghalf[:], -0.5), writes=[r("neghalf")])
        DMA("pool", I("dma_start", out=w2aug[0:16, :], in_=w_a2), writes=[r("w2aug")])
        DMA("pool", I("dma_start", out=w2aug[32:33, :], in_=b_al), writes=[r("w2aug")])
        POOL(I("memset", a_aug[:], 0.0), writes=[r("a_aug")])
        POOL(I("memset", a_aug[32:33, :], 1.0), writes=[r("a_aug")])
        POOL(I("memset", p_hist[:], 0.0), writes=[r("p_hist")])
        DMA("pool", I("dma_start", out=p_hist[113:128, :], in_=sp_in[0]), writes=[r("p_hist")])
        DMA("pool", I("dma_start", out=p_hist[49:64, :], in_=sp_in[1]), writes=[r("p_hist")])

        def cast_slab(i):
            if i < 11:
                c0 = IN_SLABS[i]
                DMA("pool", I("dma_start", out=wsl[i], in_=w_in[:, c0:c0 + 512].rearrange("(k p) c -> p k c", p=128)),
                    writes=[R_wsl[i]])
            elif i < 22:
                jj = i - 11
                DMA("pool", I("dma_start", out=wsl[i][:, :, 0:256], in_=w_fg[:, jj * 256:(jj + 1) * 256].rearrange("(k p) c -> p k c", p=128)),
                    writes=[R_wsl[i]])
                DMA("pool", I("dma_start", out=wsl[i][:, :, 256:512], in_=w_fu[:, jj * 256:(jj + 1) * 256].rearrange("(k p) c -> p k c", p=128)),
                    writes=[R_wslb[i]])
            else:
                half, jj = divmod(i - 22, 3)
                n = 8 if jj < 2 else 6
                src = w_fd.rearrange("(f p) c -> p f c", p=128)[:, jj * 8:jj * 8 + n, half * 512:(half + 1) * 512]
                DMA("pool", I("dma_start", out=wsl[i][:, 0:n, :], in_=src), writes=[R_wsl[i]])

        POOL(I("memset", S_f[1][:], 0.0), writes=R_S[1])
        POOL(I("memset", S_ring[0][:], 0.0), writes=R_ring[0])
        DMA("sp", I("dma_start", out=S_f[0][:], in_=sg_in[0].rearrange("h k v -> k h v")), writes=R_S[0], key="S0")
        DMA("sp", I("dma_start", out=S_f[2][:], in_=sg_in[1].rearrange("h k v -> k h v")), writes=R_S[2], key="S2")
        ACT(I("activation", out=S_b[0][:], in_=S_f[0][:], func=AF.Copy), reads=R_S[0], writes=[R_Sb[0]])
        ACT(I("activation", out=S_b[2][:], in_=S_f[2][:], func=AF.Copy), reads=R_S[2], writes=[R_Sb[2]])
        load_x(0)
        for i in range(11):
            cast_slab(i)
        for kc in range(8):
            DMA("pool", I("dma_start", out=wout_sb[:, kc, :], in_=w_out[kc * 128:(kc + 1) * 128, :]), writes=[r("wout%d" % kc)])
        cast_done = [11]

        def cast_upto(k):
            while cast_done[0] <= k and cast_done[0] < N_SLABS:
                cast_slab(cast_done[0])
                cast_done[0] += 1


        slab_seq = [0]
        loaded = [0]
        total_slabs = len(tiles) * N_SLABS

        def ensure(idx):
            while loaded[0] <= idx and loaded[0] < total_slabs:
                i = loaded[0]
                sid = i % N_SLABS
                b = i % 3
                nv = 6 if sid in (24, 27) else 8
                DMA("sp", I("dma_start", out=slab[b][:, 0:nv, :], in_=wsl[sid][:, 0:nv, :]),
                    reads=[R_wsl[sid], R_wslb[sid]], writes=[R_slab[b]], key="slab%d" % b)
                loaded[0] += 1

        def next_slab(prefetch=2):
            idx = slab_seq[0]
            if idx < N_SLABS:
                cast_upto(idx + 4)
            ensure(idx + prefetch)
            slab_seq[0] += 1
            b = idx % 3
            return slab[b], R_slab[b]

        psum_tok = [(bank0[:], [RB[0]]), (bank1[:], [RB[1]]), (pr23[:, 0:512], [RB[2]]), (pr23[:, 512:1024], [RB[3]])]
        tok_ctr = [0]

        def next_tok_psum():
            pr = psum_tok[tok_ctr[0] % 4]
            tok_ctr[0] += 1
            return pr

        def rms_rstd(src_ss, dst, c0, c1, scale, rs, ws):
            ACT(I("activation", out=dst[:, c0:c1], in_=src_ss[:, c0:c1], func=AF.Ln, scale=scale, bias=EPS), reads=rs, writes=ws)
            ACT(I("activation", out=dst[:, c0:c1], in_=dst[:, c0:c1], func=AF.Exp, scale=-0.5), reads=ws, writes=ws)

        def rms_rstd_pow(src_ss, dst, c0, c1, scale, rs, ws):
            POOL(I("tensor_scalar", out=dst[:, c0:c1], in0=src_ss[:, c0:c1], scalar1=scale, scalar2=EPS, op0=ALU.mult, op1=ALU.add), reads=rs, writes=ws)
            POOL(I("tensor_tensor", out=dst[:, c0:c1], in0=dst[:, c0:c1], in1=neghalf[:, 0:c1 - c0], op=ALU.pow), reads=ws + [r("neghalf")], writes=ws)

        def nt_act(xh_t, R_xh_t, j, rstd_t, rres, par):
            ACT(I("activation", out=xb2[par][:], in_=xh_t[:, j, :], func=AF.Copy, scale=rstd_t[:, j:j + 1]),
                reads=[R_xh_t[j]] + rres, writes=[R_xb[par]])

        def nt_pe(j, gsb, gres, par):
            pT = bank0_bf.rearrange("p (k t) -> p k t", k=8)
            for k in range(8):
                PE(I("transpose", out=pT[:, k, :], in_=xb2[par][:, k * 128:(k + 1) * 128], identity=ident[:]),
                   reads=[R_xb[par], r("ident")], writes=[RB[0]])
            DVE(I("tensor_tensor", out=xnT[:, :, j * 128:(j + 1) * 128], in0=pT,
                  in1=gsb[:].unsqueeze(2).to_broadcast([128, 8, 128]), op=ALU.mult),
                reads=[RB[0], gres], writes=[R_xnT[j]])

        def phase_AB12(ti):
            blocks = tiles[ti]
            nb = len(blocks)
            N = nb * 128
            xh, R_xh = xh2[ti % 2], R_xh2[ti % 2]
            for j in range(nb):
                ACT(I("activation", out=junk[:], in_=xh[:, j, :], func=AF.Square, accum_out=ssA[:, j:j + 1]),
                    reads=[R_xh[j]], writes=[R_ssA[j]] + R_junk)
                yield "early"
            rms_rstd(ssA, rstdA, 0, nb, 1.0 / D, R_ssA[0:nb], [r("rstdA")])
            yield "early"
            nt_act(xh, R_xh, 0, rstdA, [r("rstdA")], 0)
            yield "early_last"
            for j in range(nb):
                if j + 1 < nb:
                    nt_act(xh, R_xh, j + 1, rstdA, [r("rstdA")], (j + 1) % 2)
                nt_pe(j, g1_sb, r("g1"), j % 2)
                yield
            pA = pr23[0:16, 0:N]
            for kc in range(8):
                PE(I("matmul", pA, lhsT=wa_sb[:, kc, :], rhs=xnT[:, kc, 0:N], start=(kc == 0), stop=(kc == 7)),
                   reads=[r("wa")] + R_xnT[0:nb], writes=[RB[2]])
            ACT(I("activation", out=a_aug[0:16, 0:N], in_=pA, func=AF.Copy), reads=[RB[2]], writes=[r("a_aug")])
            yield
            for j, b in enumerate(blocks):
                smp = b["kind"] == "sample"
                tri = cf_sb[:, 3, :] if smp else cf_sb[:, 2, :]
                pZ = pr23[:, 512:1024]
                PE(I("matmul", pZ, lhsT=a_aug[0:33, j * 128:(j + 1) * 128], rhs=w2aug[0:33, :], start=True, stop=True),
                   reads=[r("a_aug"), r("w2aug")], writes=[RB[3]])
                ACT(I("activation", out=gt[:], in_=pZ, func=AF.Exp, scale=-1.0), reads=[RB[3]], writes=[r("gt")])
                ACT(I("activation", out=gt[:], in_=gt[:], func=AF.Ln, bias=1.0), reads=[r("gt")], writes=[r("gt")])
                yield
                pBk = bank1[:].rearrange("p (h t) -> p h t", h=4)
                for h in range(4):
                    PE(I("matmul", pBk[:, h, :], lhsT=gt[:, h * 128:(h + 1) * 128], rhs=tri, start=True, stop=True),
                       reads=[r("gt"), r("cf")], writes=[RB[1]])
                ACT(I("activation", out=Eb[:, :, j * 128:(j + 1) * 128], in_=pBk, func=AF.Exp),
                    reads=[RB[1]], writes=RY_Eb)
                ACT(I("activation", out=Enb[:, :, j * 128:(j + 1) * 128], in_=pBk, func=AF.Exp, scale=-1.0),
                    reads=[RB[1]], writes=RY_Enb)
                DVE(I("tensor_copy", out=elast[:, j, 1, :], in_=Eb[:, :, j * 128 + 127]), reads=RY_Eb, writes=[r("elast")])
                if smp:
                    DVE(I("tensor_copy", out=elast[:, j, 0, :], in_=Eb[:, :, j * 128 + 63]), reads=RY_Eb, writes=[r("elast")])
                yield

        for _ in phase_AB12(0):
            pass

        pending_d4 = []
        for ti, blocks in enumerate(tiles):
            nb = len(blocks)
            N = nb * 128
            xh, R_xh = xh2[ti % 2], R_xh2[ti % 2]
            first_of_seq = blocks[0]["kind"] == "prompt" and blocks[0]["blk"] == 0
            if ti == 1 or first_of_seq:
                if ti >= 1:
                    POOL(I("tensor_copy", out=S_f[0][:], in_=S_f[1][:]), reads=R_S[1], writes=R_S[0])
                    POOL(I("tensor_copy", out=S_ring[ring[0] % 5][:], in_=S_b[1][:]), reads=[R_Sb[1]], writes=R_ring[ring[0] % 5])
                    POOL(I("tensor_copy", out=p_sb[:, 0, :], in_=p_meta[:]), reads=[r("p_meta")], writes=[R_p[0]])

            for which in range(2):
                sl, rsl = next_slab()
                dst = qTt if which == 0 else kTt
                E = Eb if which == 0 else Enb
                RE = RY_Eb if which == 0 else RY_Enb
                sc = QSCALE if which == 0 else 1.0
                for h in range(4):
                    pq, rpq = next_tok_psum()
                    for kc in range(8):
                        PE(I("matmul", pq[:, 0:N], lhsT=sl[:, kc, h * 128:(h + 1) * 128], rhs=xnT[:, kc, 0:N],
                                                                        start=(kc == 0), stop=(kc == 7)),
                           reads=[rsl] + R_xnT[0:nb], writes=rpq)
                    DVE(I("scalar_tensor_tensor", out=dst[:, h, 0:N], in0=pq[:, 0:N], scalar=sc, in1=E[:, h, 0:N],
                                                                                           op0=ALU.mult, op1=ALU.mult),
                        reads=rpq + RE, writes=[(R_q if which == 0 else R_k)[jj][h] for jj in range(nb)])

            for fn in pending_d4:
                fn()
            del pending_d4[:]

            for c in range(2):
                sl, rsl = next_slab()
                for j in range(nb):
                    pv, rpv = next_tok_psum()
                    for kc in range(8):
                        PE(I("matmul", pv, lhsT=xnT[:, kc, j * 128:(j + 1) * 128], rhs=sl[:, kc, :],
                                                                        start=(kc == 0), stop=(kc == 7)),
                           reads=[rsl, R_xnT[j]], writes=rpv)
                    ACT(I("activation", out=v_sb[:, j, c * 512:(c + 1) * 512], in_=pv, func=AF.Copy),
                        reads=rpv, writes=[R_X[j]])
            sl, rsl = next_slab()
            for j, b in enumerate(blocks):
                pv, rpv = next_tok_psum()
                for kc in range(8):
                    PE(I("matmul", pv, lhsT=xnT[:, kc, j * 128:(j + 1) * 128], rhs=sl[:, kc, :],
                                                                    start=(kc == 0), stop=(kc == 7)),
                       reads=[rsl, R_xnT[j]], writes=rpv)
                DVE(I("tensor_copy", out=p_sb[:, j + 1, :], in_=pv), reads=rpv, writes=[R_p[j + 1]])
                last_blk = (b["kind"] == "sample") or (b["kind"] == "prompt" and b["blk"] == 15)
                if b["kind"] == "meta":
                    POOL(I("tensor_copy", out=p_meta[:], in_=p_sb[:, j + 1, :]), reads=[R_p[j + 1]], writes=[r("p_meta")])
                if last_blk:
                    DVE(I("tensor_copy", out=p_f32, in_=pv), reads=rpv, writes=[r("ybuf")])
                    if b["kind"] == "sample":
                        DMA("pool", I("dma_start", out=sps[0], in_=p_f32[49:64, :]), reads=[r("ybuf")], key="pf")
                        DMA("pool", I("dma_start", out=sps[1], in_=p_f32[113:128, :]), reads=[r("ybuf")], key="pf")
                    else:
                        DMA("pool", I("dma_start", out=spp[b["seq"]], in_=p_f32[113:128, :]), reads=[r("ybuf")], key="pf")

            if ti + 1 < len(tiles):
                load_x(ti + 1)

            blk_info = []
            for j, b in enumerate(blocks):
                smp = b["kind"] == "sample"
                js = slice(j * 128, (j + 1) * 128)
                mask = cf_sb[:, 1, :] if smp else cf_sb[:, 0, :]
                if b["kind"] == "meta" or smp:
                    pprev, rpprev = p_hist[:], r("p_hist")
                else:
                    pprev, rpprev = p_sb[:, j, :], R_p[j]
                mc0 = 8 if smp else 0
                mp0 = 12 if smp else 4
                par = j % 2
                pAt = (bank0 if par == 0 else bank1)[:].rearrange("p (h t) -> p h t", h=4)
                rAt = [RB[par]]
                a4 = attn4[:, j, :].rearrange("p (h t) -> p h t", h=4)
                for h in range(4):
                    PE(I("matmul", pAt[:, h, :], lhsT=kTt[:, h, js], rhs=qTt[:, h, js], start=True, stop=True),
                       reads=[R_q[j][h], R_k[j][h]], writes=rAt)
                DVE(I("tensor_tensor", out=a4, in0=pAt, in1=mask.unsqueeze(1).to_broadcast([128, 4, 128]), op=ALU.mult),
                    reads=rAt + [r("cf")], writes=[R_attn[j]])
                pKt = pr23[:, par * 512:(par + 1) * 512].bitcast(BF16)[:, 0:512].rearrange("p (h t) -> p h t", h=4)
                rKt = [RB[2 + par]]
                k4 = ktok4[:, j, :].rearrange("p (h t) -> p h t", h=4)
                for h in range(4):
                    PE(I("transpose", out=pKt[:, h, :], in_=kTt[:, h, js], identity=ident[:]),
                       reads=[R_k[j][h], r("ident")], writes=rKt)
                ACT(I("activation", out=k4, in_=pKt, func=AF.Copy), reads=rKt, writes=[R_ktok[j]])
                pMix = pr45[:, par * 512:(par + 1) * 512].rearrange("p (g t) -> p g t", g=4)
                rMix = [RB[4 + par]]
                m4 = mixT4[:, j, :].rearrange("p (g t) -> p g t", g=4)
                for g in range(4):
                    gs = slice(g * 128, (g + 1) * 128)
                    PE(I("matmul", pMix[:, g, :], lhsT=pprev[:, gs], rhs=Mmat[:, mp0 + g, :], start=True, stop=False),
                       reads=[rpprev, r("Mmat")], writes=rMix)
                    PE(I("matmul", pMix[:, g, :], lhsT=p_sb[:, j + 1, gs], rhs=Mmat[:, mc0 + g, :], start=False, stop=True),
                       reads=[R_p[j + 1], r("Mmat")], writes=rMix)
                ACT(I("activation", out=m4, in_=pMix, func=AF.Copy), reads=rMix, writes=[R_mixT[j]])
                blk_info.append(dict(a4=a4, k4=k4, m4=m4, smp=smp))

            def state_step(j, b):
                smp = b["kind"] == "sample"
                k4 = blk_info[j]["k4"]
                if smp:
                    POOL(I("tensor_copy", out=S_b[1][:], in_=S_ring[ring[0] % 5][:]), reads=R_ring[ring[0] % 5], writes=[R_Sb[1]])
                    sts = [(0, slice(0, 64), 0), (2, slice(64, 128), 1)]
                    blk_info[j]["S_in"] = [(S_b[0], [R_Sb[0]] * 4, slice(0, 64)), (S_b[2], [R_Sb[2]] * 4, slice(64, 128))]
                else:
                    sts = [(1 if b["kind"] == "meta" else 0, slice(0, 128), 1)]
                    blk_info[j]["S_in"] = [(S_ring[ring[0] % 5], R_ring[ring[0] % 5], slice(0, 128))]
                pS = pr67[:].rearrange("p (h v) -> p h v", h=4)
                rbs = [RB[6], RB[7]]
                for (si, psl, wh) in sts:
                    for h in range(4):
                        hs = slice(h * 256, (h + 1) * 256)
                        PE(I("matmul", pS[:, h, :], lhsT=k4[psl, h, :], rhs=v_sb[psl, j, hs], start=True, stop=True),
                           reads=[R_ktok[j], R_X[j]], writes=[rbs[h // 2]])
                    DVE(I("tensor_tensor", out=S_f[si][:], in0=S_f[si][:], in1=pS, op=ALU.add),
                        reads=rbs + R_S[si], writes=R_S[si])
                    if not smp:
                        so = (ring[0] + 1) % 5
                        for h in range(4):
                            ACT(I("activation", out=S_ring[so][:, h, :], in_=S_f[si][:, h, :], func=AF.Copy, scale=elast[:, j, wh, h:h + 1]),
                                reads=[R_S[si][h], r("elast")], writes=[R_ring[so][h]])
                    for h in range(4):
                        DVE(I("tensor_scalar", out=S_f[si][:, h, :], in0=S_f[si][:, h, :], scalar1=elast[:, j, wh, h:h + 1], scalar2=None, op0=ALU.mult),
                            reads=[R_S[si][h], r("elast")], writes=[R_S[si][h]])
                if not smp:
                    ring[0] += 1
                if smp:
                    DMA("pool", I("dma_start", out=sgs[0].rearrange("h k v -> k h v"), in_=S_f[0][:]), reads=R_S[0], key="So0")
                    DMA("pool", I("dma_start", out=sgs[1].rearrange("h k v -> k h v"), in_=S_f[2][:]), reads=R_S[2], key="So2")
                elif b["kind"] == "prompt" and b["blk"] == 15:
                    DMA("pool", I("dma_start", out=sgp[b["seq"]].rearrange("h k v -> k h v"), in_=S_f[0][:]), reads=R_S[0], key="So0")

            slab_i = 0
            hbs = [(bufA[:, 0:512], RY_bufA[0:2]), (bufA[:, 512:1024], RY_bufA[2:4]),
                   (bufB[:, 0:512], RY_bufB[0:2]), (bufB[:, 512:1024], RY_bufB[2:4])]
            hbs16 = [(hb_[:, 0:256].bitcast(BF16), rr_) for (hb_, rr_) in hbs]
            hctr = 0
            for kind in range(2):
                for c in range(2):
                    sl, rsl = next_slab()
                    for j in range(nb):
                        pv, rpv = next_tok_psum()
                        for kc in range(8):
                            PE(I("matmul", pv, lhsT=xnT[:, kc, j * 128:(j + 1) * 128], rhs=sl[:, kc, :],
                                 start=(kc == 0), stop=(kc == 7)),
                               reads=[rsl, R_xnT[j]], writes=rpv)
                        cs = slice(c * 512, (c + 1) * 512)
                        hb, rhb = hbs[hctr % 4]
                        hctr += 1
                        if kind == 0:
                            ACT(I("activation", out=hb, in_=pv, func=AF.Sigmoid), reads=rpv, writes=rhb)
                            DVE(I("tensor_tensor", out=Ga[:, j, cs], in0=pv, in1=hb, op=ALU.mult),
                                reads=rpv + rhb, writes=[R_X[4 + j]])
                            POOL(I("tensor_tensor", out=Ga[:, j, cs], in0=Ga[:, j, cs], in1=gam_rep[:, cs], op=ALU.mult),
                                 reads=[R_X[4 + j], r("gam")], writes=[R_X[4 + j]])
                        else:
                            ACT(I("activation", out=hb, in_=pv, func=AF.Sigmoid), reads=rpv, writes=rhb)
                            DVE(I("tensor_tensor", out=Ga[:, j, cs], in0=Ga[:, j, cs], in1=hb, op=ALU.mult),
                                reads=rhb + [R_X[4 + j]], writes=[R_X[4 + j]])
                    if slab_i < nb and slab_i != 3:
                        state_step(slab_i, blocks[slab_i])
                    slab_i += 1
            gb_slabs = [next_slab(), next_slab(prefetch=1)]

            def gb_part(j, c, bk=0):
                sl, rsl = gb_slabs[c]
                pb = {0: bank0[:], 1: bank1[:], 7: pr67[:, 512:1024]}[bk]
                for kc in range(8):
                    PE(I("matmul", pb, lhsT=xnT[:, kc, j * 128:(j + 1) * 128], rhs=sl[:, kc, :],
                         start=(kc == 0), stop=(kc == 7)),
                       reads=[rsl, R_xnT[j]], writes=[RB[bk]])
                ACT(I("activation", out=Gb[:, j, c * 512:(c + 1) * 512], in_=pb, func=AF.Sigmoid),
                    reads=[RB[bk]], writes=[R_X[8 + j]])

            bufA_bf = regY[:, 0:512].bitcast(BF16)
            bufB_bf = regY[:, 1024:1536].bitcast(BF16)
            RY_Abf = R_Y[0:2]
            RY_Bbf = R_Y[4:6]

            def op_T(j):
                mg = merged2[j % 2]
                rmg = RY_merged2[j % 2]
                pMt = bank1_bf.rearrange("p (k t) -> p k t", k=8)
                for k in range(8):
                    PE(I("transpose", out=pMt[:, k, :], in_=mg[:, k * 128:(k + 1) * 128], identity=ident[:]),
                       reads=rmg + [r("ident")], writes=[RB[1]])
                ACT(I("activation", out=mT_sb, in_=pMt, func=AF.Copy), reads=[RB[1]], writes=RY_mT)

            def op_mm(j, halves=(0, 1)):
                pOut = pr67[:]
                for c in halves:
                    for kc in range(8):
                        PE(I("matmul", pOut[:, c * 512:(c + 1) * 512], lhsT=mT_sb[:, kc, :], rhs=wout_sb[:, kc, c * 512:(c + 1) * 512],
                             start=(kc == 0), stop=(kc == 7)),
                           reads=RY_mT + [r("wout%d" % kc)], writes=[RB[6 + c]])
                if 1 in halves:
                    DVE(I("tensor_tensor", out=xh[:, j, :], in0=xh[:, j, :], in1=pOut, op=ALU.add),
                        reads=[RB[6], RB[7], R_xh[j]], writes=[R_xh[j]])

            def d1_act(j):
                ACT(I("activation", out=junk[:], in_=xh[:, j, :], func=AF.Square, accum_out=ss[:, j:j + 1]),
                    reads=[R_xh[j]], writes=[R_ss[j]] + R_junk)
                rms_rstd_pow(ss, rstd, j, j + 1, 1.0 / D, [R_ss[j]], [R_ss[j]])
                nt_act(xh, R_xh, j, rstd, [R_ss[j]], j % 2)

            gb_part(0, 0)
            gb_part(0, 1, bk=1)
            if nb == 4:
                state_step(3, blocks[3])
            for j, b in enumerate(blocks):
                bi = blk_info[j]
                pO = pr23[:]
                for h in range(4):
                    hs = slice(h * 256, (h + 1) * 256)
                    PE(I("matmul", pO[:, hs], lhsT=bi["a4"][:, h, :], rhs=v_sb[:, j, hs], start=True, stop=False),
                       reads=[R_attn[j], R_X[j]], writes=[RB[2 + h // 2]])
                    for (Sb_t, Sb_r, psl) in bi["S_in"]:
                        PE(I("matmul", pO[psl, hs], lhsT=qTt[:, h, j * 128 + psl.start:j * 128 + psl.stop], rhs=Sb_t[:, h, :], start=False, stop=True),
                           reads=[R_q[j][h], Sb_r[h]], writes=[RB[2 + h // 2]])
                pYb = pr45[:]
                rYb = [RB[4], RB[5]]
                for g in range(4):
                    PE(I("matmul", pYb[:, g * 256:(g + 1) * 256], lhsT=bi["m4"][:, g, :], rhs=wpool_sb[:, g, :], start=True, stop=True),
                       reads=[R_mixT[j], r("wpool")], writes=[rYb[g // 2]])
                DVE(I("tensor_tensor", out=bufB_bf, in0=pYb, in1=Gb[:, j, :], op=ALU.mult),
                    reads=rYb + [R_X[8 + j]], writes=RY_Bbf)
                fill = [q for q in {0: [1, 2], 1: [3]}.get(j, []) if q < nb]
                early_T = (j >= 1 and not fill)
                if early_T:
                    op_T(j - 1)
                for h in range(4):
                    hs = slice(h * 256, (h + 1) * 256)
                    ACT(I("activation", out=junk[:, hs], in_=pO[:, hs], func=AF.Square, accum_out=ssq[:, h:h + 1]),
                        reads=[RB[2 + h // 2]], writes=[R_ssq[h], R_junk[h]])
                rms_rstd_pow(ssq, rsq, 0, 4, 1.0 / 256, R_ssq, [r("rsq")])
                for h in range(4):
                    hs = slice(h * 256, (h + 1) * 256)
                    DVE(I("scalar_tensor_tensor", out=bufA_bf[:, hs], in0=pO[:, hs], scalar=rsq[:, h:h + 1], in1=Ga[:, j, hs],
                          op0=ALU.mult, op1=ALU.mult),
                        reads=[RB[2 + h // 2], r("rsq"), R_X[4 + j]], writes=[RY_Abf[h // 2]])
                DVE(I("tensor_tensor", out=merged2[j % 2], in0=bufA_bf, in1=bufB_bf, op=ALU.add), reads=RY_Abf + RY_Bbf, writes=RY_merged2[j % 2])
                if fill:
                    gb_part(fill[0], 0)
                if j >= 1 and not early_T:
                    op_T(j - 1)
                if fill:
                    gb_part(fill[0], 1, bk=7)
                for q in fill[1:]:
                    gb_part(q, 0)
                    gb_part(q, 1, bk=7)
                if j >= 1:
                    op_mm(j - 1)
                if j >= 2:
                    d1_act(j - 2)
                if j >= 3:
                    nt_pe(j - 3, g2_sb, r("g2"), (j - 3) % 2)
            op_T(nb - 1)
            if nb >= 2:
                d1_act(nb - 2)
            if nb >= 3:
                nt_pe(nb - 3, g2_sb, r("g2"), (nb - 3) % 2)
            op_mm(nb - 1, halves=(0,))
            if nb >= 2:
                nt_pe(nb - 2, g2_sb, r("g2"), (nb - 2) % 2)
            op_mm(nb - 1, halves=(1,))
            d1_act(nb - 1)
            N1 = N - 128
            sl0, rsl0 = next_slab()
            ebanks = [(bank1[:], [RB[1]]), (pr23[:, 0:512], [RB[2]]), (pr23[:, 512:1024], [RB[3]]), (pr45[:, 0:512], [RB[4]])]

            def gu_part(c0, c1, rres):
                for cc in range(2):
                    for which in range(2):
                        pb, rb = ebanks[2 * cc + which]
                        w0 = which * 256 + cc * 128
                        for kc in range(8):
                            PE(I("matmul", pb[:, c0:c1], lhsT=sl0[:, kc, w0:w0 + 128], rhs=xnT[:, kc, c0:c1],
                                 start=(kc == 0), stop=(kc == 7)),
                               reads=[rsl0] + rres, writes=rb)

            if N1 > 0:
                gu_part(0, N1, R_xnT[0:nb - 1])
            nt_pe(nb - 1, g2_sb, r("g2"), (nb - 1) % 2)
            gu_part(N1, N, [R_xnT[nb - 1]])
            for cc in range(2):
                pG, rG = ebanks[2 * cc]
                pU, rU = ebanks[2 * cc + 1]
                ACT(I("activation", out=sg_t[:, 0:N], in_=pG[:, 0:N], func=AF.Sigmoid), reads=rG, writes=RY_sg)
                DVE(I("tensor_tensor", out=sg_t[:, 0:N], in0=pG[:, 0:N], in1=sg_t[:, 0:N], op=ALU.mult), reads=rG + RY_sg, writes=RY_sg)
                DVE(I("tensor_tensor", out=uT[:, cc, 0:N], in0=pU[:, 0:N], in1=sg_t[:, 0:N], op=ALU.mult),
                    reads=rU + RY_sg, writes=[R_X[0]])

            if blocks[-1]["kind"] == "prompt" and blocks[-1]["blk"] != 15:
                POOL(I("tensor_copy", out=p_sb[:, 0, :], in_=p_sb[:, nb, :]), reads=[R_p[nb]], writes=[R_p[0]])

            gu = [(bank0[:], [RB[0]]), (bank1[:], [RB[1]]), (pr23[:, 0:512], [RB[2]]), (pr23[:, 512:1024], [RB[3]])]
            gctr = 0
            hoist = None
            early_done = True
            for jj in range(1, 11):
                sl, rsl = next_slab()
                if jj == 7 and ti + 1 < len(tiles):
                    hoist = phase_AB12(ti + 1)
                    early_done = False
                for cc in range(2):
                    ffc = jj * 2 + cc
                    if hoist is not None and not early_done:
                        if next(hoist, None) == "early_last":
                            early_done = True
                    pG, rG = gu[gctr % 4]
                    pU, rU = gu[(gctr + 1) % 4]
                    gctr += 2
                    for kc in range(8):
                        PE(I("matmul", pG[:, 0:N], lhsT=sl[:, kc, cc * 128:(cc + 1) * 128], rhs=xnT[:, kc, 0:N],
                                                                          start=(kc == 0), stop=(kc == 7)),
                           reads=[rsl] + R_xnT[0:nb], writes=rG)
                    for kc in range(8):
                        PE(I("matmul", pU[:, 0:N], lhsT=sl[:, kc, 256 + cc * 128:256 + (cc + 1) * 128], rhs=xnT[:, kc, 0:N],
                                                                          start=(kc == 0), stop=(kc == 7)),
                           reads=[rsl] + R_xnT[0:nb], writes=rU)
                    ACT(I("activation", out=sg_t[:, 0:N], in_=pG[:, 0:N], func=AF.Sigmoid), reads=rG, writes=RY_sg)
                    DVE(I("tensor_tensor", out=sg_t[:, 0:N], in0=pG[:, 0:N], in1=sg_t[:, 0:N], op=ALU.mult), reads=rG + RY_sg, writes=RY_sg)
                    DVE(I("tensor_tensor", out=uT[:, ffc, 0:N], in0=pU[:, 0:N], in1=sg_t[:, 0:N], op=ALU.mult),
                        reads=rU + RY_sg, writes=[R_X[ffc // 2]])
            while hoist is not None and not early_done:
                if next(hoist, None) == "early_last":
                    early_done = True
            dacc = [(pr45[:, 0:512], [RB[4]]), (pr45[:, 512:1024], [RB[5]]), (pr67[:, 0:512], [RB[6]]), (pr67[:, 512:1024], [RB[7]])]
            for half in range(2):
                for jj in range(3):
                    sl, rsl = next_slab()
                    n = 8 if jj < 2 else 6
                    for i in range(n):
                        ffc = jj * 8 + i
                        if hoist is not None and i % 2 == 1:
                            next(hoist, None)
                        for j in range(nb):
                            PE(I("matmul", dacc[j][0], lhsT=uT[:, ffc, j * 128:(j + 1) * 128], rhs=sl[:, i, :],
                                                                            start=(ffc == 0), stop=(ffc == NFF - 1)),
                               reads=[rsl, R_X[ffc // 2]], writes=dacc[j][1])
                for j in range(nb):
                    hsl = slice(half * 512, (half + 1) * 512)
                    DVE(I("tensor_tensor", out=xh[:, j, hsl], in0=xh[:, j, hsl], in1=dacc[j][0], op=ALU.add),
                        reads=dacc[j][1] + [R_xh[j]], writes=[R_xh[j]])
            if hoist is not None:
                for _ in hoist:
                    pass
            def final_norm(ti=ti, blocks=blocks, xh=xh, R_xh=R_xh):
                for j, b in enumerate(blocks):
                    if b["kind"] == "meta":
                        continue
                    ACT(I("activation", out=junk[:], in_=xh[:, j, :], func=AF.Square, accum_out=ss[:, j:j + 1]),
                        reads=[R_xh[j]], writes=[R_ss[j]] + R_junk)
                    rms_rstd(ss, rstd, j, j + 1, 1.0 / D, [R_ss[j]], [R_ss[j]])
                    DVE(I("scalar_tensor_tensor", out=xh[:, j, :], in0=xh[:, j, :], scalar=rstd[:, j:j + 1], in1=gf_rep[:], op0=ALU.mult, op1=ALU.mult),
                        reads=[R_xh[j], R_ss[j], r("gfr")], writes=[R_xh[j]])
                    if b["kind"] == "sample":
                        DMA("pool", I("dma_start", out=ys, in_=xh[:, j, :]), reads=[R_xh[j]], key="yout%d_%d" % (ti % 2, j))
                    else:
                        t0 = b["blk"] * 128
                        DMA("pool", I("dma_start", out=yp[b["seq"], t0:t0 + 128, :], in_=xh[:, j, :]), reads=[R_xh[j]], key="yout%d_%d" % (ti % 2, j))

            pending_d4.append(final_norm)

        for fn in pending_d4:
            fn()
        del pending_d4[:]

        P.emit(st)
    return nc


_NC_CACHE = {}


def _get_nc(limit=None):
    if limit not in _NC_CACHE:
        _NC_CACHE[limit] = build_program(limit)
    return _NC_CACHE[limit]


def kernel(x_prompt, x_sample, state_gla, state_pool, meta_tokens, norm1_g, w_in, w_alpha2, b_alpha,
           gla_norm_g, w_pool, pool_scale, w_out, norm2_g, w_ffn_gate, w_ffn_up, w_ffn_down, norm_f_g,
           _limit=None):
    f = lambda a: np.ascontiguousarray(np.asarray(a, dtype=np.float32))
    cf, cb = _consts()
    nc = _get_nc(_limit)
    shared = {
        "meta": f(meta_tokens),
        "g1c": f(np.asarray(norm1_g).reshape(8, 128).T),
        "g2c": f(np.asarray(norm2_g).reshape(8, 128).T),
        "w_in": f(w_in), "w_a2": f(w_alpha2), "b_al": f(np.asarray(b_alpha).reshape(1, 512)),
        "gla_g": f(gla_norm_g), "w_pool": f(w_pool), "pscale": f(pool_scale), "w_out": f(w_out),
        "w_fg": f(w_ffn_gate), "w_fu": f(w_ffn_up), "w_fd": f(w_ffn_down), "gf": f(norm_f_g),
        "cst_f": cf, "cst_b": cb,
    }
    x_prompt = np.asarray(x_prompt)
    x_sample = np.asarray(x_sample)
    state_gla = np.asarray(state_gla)
    state_pool = np.asarray(state_pool)
    in_maps = []
    for c in range(8):
        m = dict(shared)
        m["xp"] = f(x_prompt[4 * c:4 * c + 4])
        m["xs"] = f(x_sample[2 * c:2 * c + 2].reshape(128, D))
        m["sg_in"] = f(state_gla[2 * c:2 * c + 2])
        m["sp_in"] = f(state_pool[2 * c:2 * c + 2])
        in_maps.append(m)
    res = run_bass_kernel_spmd(nc, in_maps, core_ids=list(range(8)))
    rs = res.results
    y_prompt = np.concatenate([rs[c]["yp"] for c in range(8)], axis=0).astype(np.float32)
    y_sample = np.concatenate([rs[c]["ys"].reshape(2, 64, D) for c in range(8)], axis=0).astype(np.float32)
    sgp = np.concatenate([rs[c]["sgp"] for c in range(8)], axis=0).astype(np.float32)
    spp = np.concatenate([rs[c]["spp"] for c in range(8)], axis=0).astype(np.float32)
    sgs = np.concatenate([rs[c]["sgs"] for c in range(8)], axis=0).astype(np.float32)
    sps = np.concatenate([rs[c]["sps"] for c in range(8)], axis=0).astype(np.float32)
    return (y_prompt, y_sample, sgp, spp, sgs, sps)
```
